# Optimizing a Trainium2 kernel written in Bass

```python
import jax, jax.numpy as jnp
from jax import lax
import numpy as np

D_MODEL = 1024
BATCH = 2
SEQ = 8192
DEPTH = 2
DEC_BATCH = 8
DEC_SEQ = 2048
PAST_LEN = 128

HEAD_DIM = 64
N_ATTN_HEADS = 8
N_KV_HEADS = 2
ATTN_WIDTH = N_ATTN_HEADS * HEAD_DIM
KV_WIDTH = N_KV_HEADS * HEAD_DIM
N_GMLP_HEADS = 8
GMLP_WIDTH = N_GMLP_HEADS * HEAD_DIM
MIX_WIDTH = ATTN_WIDTH + GMLP_WIDTH
IN_PROJ_WIDTH = ATTN_WIDTH + 2 * KV_WIDTH + 2 * GMLP_WIDTH
WINDOW = 128
BLOCK = 128
CHUNK = 128
D_FF = 2816
EPS = 1e-6

kernel_name = "hymba_swa_gmlp_macaron_encoder"


def rms_norm(x, gain):
    x32 = x.astype(jnp.float32)
    y = x32 * lax.rsqrt(jnp.mean(x32 * x32, axis=-1, keepdims=True) + EPS)
    return (y * gain.astype(jnp.float32)).astype(x.dtype)


def swiglu(x, w_gate, w_up, w_down):
    return (jax.nn.silu(x @ w_gate) * (x @ w_up)) @ w_down


def windowed_gqa(q, k, v, sink, slopes):
    b, s = q.shape[0], q.shape[1]
    nb = s // BLOCK
    g = N_ATTN_HEADS // N_KV_HEADS
    qb = q.reshape(b, nb, BLOCK, N_KV_HEADS, g, HEAD_DIM)

    def band(t):
        tp = jnp.pad(t, ((0, 0), (BLOCK, BLOCK), (0, 0), (0, 0)))
        tp = tp.reshape(b, nb + 2, BLOCK, N_KV_HEADS, HEAD_DIM)
        return jnp.concatenate([tp[:, :-2], tp[:, 1:-1], tp[:, 2:]], axis=2)

    kb, vb = band(k), band(v)
    scores = jnp.einsum('bnqkgd,bnskd->bnkgqs', qb, kb).astype(jnp.float32) * (HEAD_DIM ** -0.5)
    qi = jnp.arange(BLOCK)[:, None]
    kj = jnp.arange(3 * BLOCK)[None, :]
    rel = kj - BLOCK - qi
    kpos = (jnp.arange(nb) * BLOCK)[:, None, None] - BLOCK + kj[None]
    valid = (jnp.abs(rel) <= WINDOW)[None] & (kpos >= 0) & (kpos < s)
    dist = jnp.abs(rel).astype(jnp.float32)
    alibi = -slopes.astype(jnp.float32).reshape(N_KV_HEADS, g)[:, :, None, None] * dist
    scores = jnp.where(valid[None, :, None, None], scores + alibi[None, None], -jnp.inf)
    sink_l = sink.astype(jnp.float32).reshape(N_KV_HEADS, g)[None, None, :, :, None, None]
    m = jnp.maximum(jnp.max(scores, axis=-1, keepdims=True), sink_l)
    p = jnp.exp(scores - m)
    denom = jnp.sum(p, axis=-1, keepdims=True) + jnp.exp(sink_l - m)
    probs = (p / denom).astype(v.dtype)
    out = jnp.einsum('bnkgqs,bnskd->bnqkgd', probs, vb)
    return out.reshape(b, s, ATTN_WIDTH)


def chunked_spatial_gating(u, gv, v_gain, w_s, b_s):
    b, s = u.shape[0], u.shape[1]
    nc = s // CHUNK
    u = jax.nn.gelu(u)
    gv = jax.nn.gelu(gv)
    gh = gv.reshape(b, s, N_GMLP_HEADS, HEAD_DIM)
    gh = rms_norm(gh, v_gain.reshape(N_GMLP_HEADS, HEAD_DIM))
    gh = gh.reshape(b, nc, CHUNK, N_GMLP_HEADS, HEAD_DIM)
    mixed = jnp.einsum('hts,bnshd->bnthd', w_s.astype(gh.dtype), gh)
    mixed = mixed + b_s.T.astype(gh.dtype)[None, None, :, :, None]
    out = u.reshape(b, nc, CHUNK, N_GMLP_HEADS, HEAD_DIM) * mixed
    return out.reshape(b, s, GMLP_WIDTH)


def trunk(x, norm_ffn1, w1_gate, w1_up, w1_down, norm_mix, w_in, sink, gmlp_v_gain,
          w_spatial, b_spatial, w_out, norm_ffn2, w2_gate, w2_up, w2_down, norm_final):
    b, s, _ = x.shape
    slopes = 2.0 ** (-8.0 * jnp.arange(1, N_ATTN_HEADS + 1, dtype=jnp.float32) / N_ATTN_HEADS)
    o1 = ATTN_WIDTH
    o2 = o1 + KV_WIDTH
    o3 = o2 + KV_WIDTH
    o4 = o3 + GMLP_WIDTH
    for l in range(DEPTH):
        h = rms_norm(x, norm_ffn1[l])
        x = x + 0.5 * swiglu(h, w1_gate[l], w1_up[l], w1_down[l])
        h = rms_norm(x, norm_mix[l])
        z = h @ w_in[l]
        q = z[..., :o1].reshape(b, s, N_ATTN_HEADS, HEAD_DIM)
        k = z[..., o1:o2].reshape(b, s, N_KV_HEADS, HEAD_DIM)
        v = z[..., o2:o3].reshape(b, s, N_KV_HEADS, HEAD_DIM)
        u = z[..., o3:o4]
        gv = z[..., o4:]
        a = windowed_gqa(q, k, v, sink[l], slopes)
        c = chunked_spatial_gating(u, gv, gmlp_v_gain[l], w_spatial[l], b_spatial[l])
        x = x + jnp.concatenate([a, c], axis=-1) @ w_out[l]
        h = rms_norm(x, norm_ffn2[l])
        x = x + 0.5 * swiglu(h, w2_gate[l], w2_up[l], w2_down[l])
    return rms_norm(x, norm_final)


def setup_inputs(seed: int = 0) -> dict:
    key = jax.random.key(seed)
    ks = jax.random.split(key, 20)
    f32 = jnp.float32

    def nrm(k, shape, scale):
        return jax.random.normal(k, shape, f32) * scale

    def gain(k, shape):
        return jnp.ones(shape, f32) + 0.02 * jax.random.normal(k, shape, f32)

    return {
        "x_prompt": jax.random.normal(ks[0], (BATCH, SEQ, D_MODEL), f32),
        "x_sample": jax.random.normal(ks[1], (DEC_BATCH, DEC_SEQ, D_MODEL), f32),
        "norm_ffn1": gain(ks[2], (DEPTH, D_MODEL)),
        "w1_gate": nrm(ks[3], (DEPTH, D_MODEL, D_FF), D_MODEL ** -0.5),
        "w1_up": nrm(ks[4], (DEPTH, D_MODEL, D_FF), D_MODEL ** -0.5),
        "w1_down": nrm(ks[5], (DEPTH, D_FF, D_MODEL), D_FF ** -0.5),
        "norm_mix": gain(ks[6], (DEPTH, D_MODEL)),
        "w_in": nrm(ks[7], (DEPTH, D_MODEL, IN_PROJ_WIDTH), D_MODEL ** -0.5),
        "sink": nrm(ks[8], (DEPTH, N_ATTN_HEADS), 0.5),
        "gmlp_v_gain": gain(ks[9], (DEPTH, GMLP_WIDTH)),
        "w_spatial": nrm(ks[10], (DEPTH, N_GMLP_HEADS, CHUNK, CHUNK), CHUNK ** -0.5),
        "b_spatial": gain(ks[11], (DEPTH, N_GMLP_HEADS, CHUNK)),
        "w_out": nrm(ks[12], (DEPTH, MIX_WIDTH, D_MODEL), MIX_WIDTH ** -0.5),
        "norm_ffn2": gain(ks[13], (DEPTH, D_MODEL)),
        "w2_gate": nrm(ks[14], (DEPTH, D_MODEL, D_FF), D_MODEL ** -0.5),
        "w2_up": nrm(ks[15], (DEPTH, D_MODEL, D_FF), D_MODEL ** -0.5),
        "w2_down": nrm(ks[16], (DEPTH, D_FF, D_MODEL), D_FF ** -0.5),
        "norm_final": gain(ks[17], (D_MODEL,)),
    }


def reference(x_prompt, x_sample, norm_ffn1, w1_gate, w1_up, w1_down, norm_mix, w_in, sink,
              gmlp_v_gain, w_spatial, b_spatial, w_out, norm_ffn2, w2_gate, w2_up, w2_down,
              norm_final):
    y_prompt = trunk(x_prompt, norm_ffn1, w1_gate, w1_up, w1_down, norm_mix, w_in, sink,
                     gmlp_v_gain, w_spatial, b_spatial, w_out, norm_ffn2, w2_gate, w2_up,
                     w2_down, norm_final)
    y_sample = trunk(x_sample, norm_ffn1, w1_gate, w1_up, w1_down, norm_mix, w_in, sink,
                     gmlp_v_gain, w_spatial, b_spatial, w_out, norm_ffn2, w2_gate, w2_up,
                     w2_down, norm_final)
    return (y_prompt, y_sample)
```

```python
import contextlib
import numpy as np
import concourse.bass as bass
import concourse.mybir as mybir
from concourse.bass_utils import run_bass_kernel_spmd

F32 = mybir.dt.float32
BF16 = mybir.dt.bfloat16
ALU = mybir.AluOpType
AF = mybir.ActivationFunctionType
AX = mybir.AxisListType

D = 1024
DFF = 2816
NF = DFF // 128
EPS = 1e-6
WIN_EXT = 1920
O_K, O_V, O_U, O_G = 512, 768, 896, 1408
TS, TP = 16, 20
GK = 0.7978845608028654
GC = 0.044715
EPOCH = 16000
DEFAULT_MIXORDER = ["T", "hTevac", "h", "S", "KV", "QUG", "normf", "Wout", "mix", "gelu", "gmlp", "PV", "ghn"]


class Res:
    __slots__ = ("name", "w", "rs", "track")

    def __init__(self, name, track=True):
        self.name = name
        self.w = None
        self.rs = []
        self.track = track


class Stream:
    __slots__ = ("name", "n", "sem")

    def __init__(self, name):
        self.name = name
        self.n = 0
        self.sem = None


class Ins:
    __slots__ = ("eng", "idx", "fn", "deps", "inc", "val", "stream", "waits", "cnt", "tag", "cost")


class Sched:
    ENG = ("pe", "act", "dve", "pool", "sp")

    def __init__(self):
        self.q = {e: [] for e in self.ENG}
        self.last_real = {e: None for e in self.ENG}
        self.last_dma = {}
        self.order = []
        self.tag = ""
        self.streams = []

    def stream(self, name):
        s = Stream(name)
        self.streams.append(s)
        return s

    def add(self, eng, fn, reads=(), writes=(), stream=None, extra_deps=(), cost=0.3):
        ins = Ins()
        ins.eng = eng
        ins.fn = fn
        ins.idx = len(self.q[eng])
        ins.inc = False
        ins.stream = stream
        ins.val = None
        ins.waits = []
        ins.tag = self.tag
        ins.cost = cost
        deps = set(d for d in extra_deps if d is not None)
        for r in reads:
            if r.w is not None:
                deps.add(r.w)
        for w in writes:
            if w.w is not None:
                deps.add(w.w)
            for x in w.rs:
                deps.add(x)
        ins.deps = deps
        for r in reads:
            if r.track:
                r.rs.append(ins)
        for w in writes:
            w.w = ins
            w.rs = []
        self.q[eng].append(ins)
        self.order.append(ins)
        if fn is not None:
            self.last_real[eng] = ins
        if stream is not None:
            stream.n += 1
            ins.val = 16 * stream.n
            self.last_dma[stream] = ins
        return ins

    def schedule_window(self, a, lat=0.06):
        win = self.order[a:]
        if not win:
            return
        inwin = set(win)
        succ = {i: [] for i in win}
        npred = {}
        for i in win:
            ps = [d for d in i.deps if d in inwin and d is not i]
            npred[i] = len(ps)
            for d in ps:
                succ[d].append(i)
        prio = {}
        for i in reversed(win):
            m = 0.0
            for sx in succ[i]:
                if prio[sx] > m:
                    m = prio[sx]
            prio[i] = m + i.cost
        ready = {e: [] for e in self.ENG}
        for i in win:
            if npred[i] == 0:
                ready[i.eng].append(i)
        fin = {}
        free = {e: 0.0 for e in self.ENG}
        newq = {e: [] for e in self.ENG}
        left = len(win)
        while left:
            best = None
            for e in self.ENG:
                cands = ready[e]
                if not cands:
                    continue
                bi = None
                bs = None
                for i in cands:
                    st = free[e]
                    for d in i.deps:
                        if d in fin:
                            t = fin[d] + (lat if d.eng != e else 0.0)
                            if t > st:
                                st = t
                    key = (st, -prio[i])
                    if bs is None or key < bs:
                        bs = key
                        bi = i
                if best is None or bs < best[0]:
                    best = (bs, bi)
            (st, _), i = best
            e = i.eng
            ready[e].remove(i)
            if i.fn is None:
                fin[i] = st
                free[e] = st
            elif i.stream is not None:
                fin[i] = st + i.cost
                free[e] = st + 0.1
            else:
                fin[i] = st + i.cost
                free[e] = fin[i]
            newq[e].append(i)
            left -= 1
            for sx in succ[i]:
                npred[sx] -= 1
                if npred[sx] == 0:
                    ready[sx.eng].append(sx)
        for e in self.ENG:
            n = len(newq[e])
            if n:
                base = len(self.q[e]) - n
                assert set(self.q[e][base:]) == set(newq[e])
                self.q[e][base:] = newq[e]
                for k, i in enumerate(newq[e]):
                    i.idx = base + k
                reals = [i for i in newq[e] if i.fn is not None]
                if reals:
                    self.last_real[e] = reals[-1]

    def barrier(self):
        lasts = [self.last_real[e] for e in self.ENG if self.last_real[e] is not None]
        lasts += list(self.last_dma.values())
        for e in self.ENG:
            self.add(e, None, extra_deps=lasts)

    def lower(self, sem_ctx):
        for e in self.ENG:
            waited_idx = {}
            waited_dma = {}
            for ins in self.q[e]:
                need = {}
                for d in ins.deps:
                    if d is ins:
                        continue
                    if d.stream is not None:
                        if d.stream is ins.stream:
                            continue
                        if waited_dma.get(d.stream, 0) >= d.val:
                            continue
                        k = ("s", d.stream)
                        if k not in need or need[k].val < d.val:
                            need[k] = d
                    else:
                        if d.eng == e and (e == "pe" or d.idx >= ins.idx):
                            continue
                        if waited_idx.get(d.eng, -1) >= d.idx:
                            continue
                        k = ("e", d.eng)
                        if k not in need or need[k].idx < d.idx:
                            need[k] = d
                for k, d in need.items():
                    if k[0] == "s":
                        waited_dma[d.stream] = d.val
                    else:
                        waited_idx[d.eng] = d.idx
                        d.inc = True
                    ins.waits.append(d)
        nep = {}
        for e in self.ENG:
            c = 0
            for ins in self.q[e]:
                if ins.stream is None and ins.inc:
                    ins.cnt = c
                    c += 1
            nep[e] = max(1, (c + EPOCH - 1) // EPOCH)
        self.esems = {e: [sem_ctx("sem_%s_%d" % (e, i)) for i in range(nep[e])] for e in self.ENG}
        for s in self.streams:
            s.sem = sem_ctx("dma_" + s.name)

    def emit(self, eng, engobj):
        for ins in self.q[eng]:
            for d in ins.waits:
                if d.stream is not None:
                    engobj.wait_ge(d.stream.sem, d.val)
                else:
                    engobj.wait_ge(self.esems[d.eng][d.cnt // EPOCH], d.cnt % EPOCH + 1)
            if ins.fn is None:
                continue
            r = ins.fn(engobj)
            if ins.stream is not None:
                r.then_inc(ins.stream.sem, 16)
            elif ins.inc:
                r.then_inc(self.esems[eng][ins.cnt // EPOCH], 1)


def build_program(cfg=None):
    cfg = cfg or {}
    PH = cfg.get('phases')
    SEGS = cfg.get('segs', 'sp')
    MIXLVL = cfg.get('mixlevel', 99)
    nc = bass.Bass("TRN2", target_bir_lowering=False)
    S = Sched()

    def din(name, shape):
        return nc.dram_tensor(name, list(shape), F32, kind="ExternalInput").ap()

    xs_d = din("xs", [TS * 128, D])
    xp_d = din("xp", [TP * 128, D])
    valid_d = din("valid", [128, TS + TP])
    wg_d = [din("w1_gate", [2, D, DFF]), din("w2_gate", [2, D, DFF])]
    wu_d = [din("w1_up", [2, D, DFF]), din("w2_up", [2, D, DFF])]
    wd_d = [din("w1_down", [2, DFF, D]), din("w2_down", [2, DFF, D])]
    gf_d = [din("g_ffn1", [2, 128, 8]), din("g_ffn2", [2, 128, 8])]
    gm_d = din("g_mix", [2, 128, 8])
    win_d = din("w_in_ext", [2, D, WIN_EXT])
    wout_d = din("w_out", [2, D, D])
    wst_d = din("w_sT", [2, 128, 8, 128])
    bb_d = din("b_bc", [2, 128, 512])
    vg_d = din("vgain", [2, 512])
    sink_d = din("sink", [2, 8])
    gfin_d = din("g_fin", [1, D])
    ys_d = nc.dram_tensor("ys", [TS * 128, D], F32, kind="ExternalOutput").ap()
    yp_d = nc.dram_tensor("yp", [16 * 128, D], F32, kind="ExternalOutput").ap()

    es = contextlib.ExitStack()
    with es:
        ARENA_BYTES = 207 * 1024
        arena = es.enter_context(nc.sbuf_tensor("arena", [128, ARENA_BYTES // 4], F32))
        banks = [es.enter_context(nc.psum_tensor("bank%d" % i, [128, 512], F32))[:] for i in range(6)]
        b67 = es.enter_context(nc.psum_tensor("bank67", [128, 1024], F32))[:]
        banks += [b67[:, 0:512], b67[:, 512:1024]]
        Rb = [Res("bank%d" % i) for i in range(8)]

        class Alloc:
            def __init__(self, base):
                self.off = base

            def get(self, shape, dt):
                n = 1
                for s in shape[1:]:
                    n *= s
                nbytes = n * (4 if dt == F32 else 2)
                nbytes = (nbytes + 63) // 64 * 64
                a = arena[:, self.off // 4:(self.off + nbytes) // 4]
                if dt != F32:
                    a = a.bitcast(dt)
                    a = a[:, 0:n]
                else:
                    a = a[:, 0:n]
                if len(shape) == 3:
                    a = a.rearrange("p (a b) -> p a b", a=shape[1])
                elif len(shape) == 4:
                    a = a.rearrange("p (a b c) -> p a b c", a=shape[1], b=shape[2])
                self.off += nbytes
                assert self.off <= ARENA_BYTES, "SBUF arena overflow %d" % self.off
                return a

        P = Alloc(0)
        x_sb = P.get([128, TP, D], F32)
        R_x = [Res("x%d" % t) for t in range(TP)]
        ident = P.get([128, 128], BF16)
        R_ident = Res("ident", track=False)
        Bhi = P.get([128, 6, 512], BF16)
        R_E = Res("E", track=False)
        valid_sb = P.get([128, TS + TP], F32)
        R_valid = Res("valid", track=False)
        vrep_sb = P.get([128, TS + TP, 64], BF16)
        R_vrep = Res("vrep", track=False)
        ss_sb = P.get([128, TP], F32)
        R_ss = Res("ss")
        ms_sb = P.get([128, TP], F32)
        R_ms = Res("ms")
        rstd_sb = P.get([128, TP], F32)
        R_rstd = Res("rstd")
        mhalf_sb = P.get([128, TP], F32)
        R_mhalf = Res("mhalf", track=False)
        gain_sb = P.get([128, 8], F32)
        R_gain = Res("gain")
        PH_BASE = P.off
        SCR = Alloc(ARENA_BYTES - 4096)
        dmat = SCR.get([128, 128], F32)
        absd = SCR.get([128, 128], F32)
        maskt = SCR.get([128, 128], F32)
        etmp = SCR.get([128, 128], F32)
        negm = SCR.get([128, 128], F32)

        st_x = S.stream("x")
        st_c = S.stream("const")
        st_w = {}

        def wstream(name):
            if name not in st_w:
                st_w[name] = S.stream(name)
            return st_w[name]

        def fsz(ap):
            n = 1
            for d in ap.shape[1:]:
                n *= d
            return n

        def mm(out, lhsT, rhs, start, stop, reads, writes):
            return S.add("pe", lambda e: e.matmul(out, lhsT=lhsT, rhs=rhs, start=start, stop=stop),
                         reads=reads, writes=writes, cost=max(fsz(out), 64) / 2400.0 + 0.003)

        def act(out, in_, func, reads, writes, scale=1.0, accum_out=None, bias=None):
            c = 0.2 + fsz(out) / 1200.0
            if bias is not None:
                return S.add("act", lambda e: e.activation(out=out, in_=in_, func=func, scale=scale, bias=bias),
                             reads=reads, writes=writes, cost=c)
            if accum_out is None:
                return S.add("act", lambda e: e.activation(out=out, in_=in_, func=func, scale=scale),
                             reads=reads, writes=writes, cost=c)
            return S.add("act", lambda e: e.activation(out=out, in_=in_, func=func, scale=scale,
                                                       accum_out=accum_out), reads=reads, writes=writes, cost=c)

        def tt(out, in0, in1, op, reads, writes, eng="dve"):
            c = (0.15 + fsz(out) / 960.0) if eng == "dve" else (0.4 + fsz(out) / 650.0)
            return S.add(eng, lambda e: e.tensor_tensor(out=out, in0=in0, in1=in1, op=op),
                         reads=reads, writes=writes, cost=c)

        def tsc(out, in0, s1, s2, op0, op1, reads, writes):
            c = 0.15 + fsz(out) / 1900.0
            if s2 is None:
                return S.add("dve", lambda e: e.tensor_scalar(out=out, in0=in0, scalar1=s1, scalar2=None, op0=op0),
                             reads=reads, writes=writes, cost=c)
            return S.add("dve", lambda e: e.tensor_scalar(out=out, in0=in0, scalar1=s1, scalar2=s2, op0=op0, op1=op1),
                         reads=reads, writes=writes, cost=c)

        def stt(out, in0, scalar, in1, op0, op1, reads, writes):
            return S.add("dve", lambda e: e.scalar_tensor_tensor(out=out, in0=in0, scalar=scalar, in1=in1,
                                                                 op0=op0, op1=op1), reads=reads, writes=writes,
                         cost=0.15 + fsz(out) / 960.0)

        def dma(eng, out, in_, reads, writes, stream, after=()):
            return S.add(eng, lambda e: e.dma_start(out=out, in_=in_), reads=reads, writes=writes, stream=stream,
                         cost=2.0 + 128 * fsz(out) * 4 / 250e3, extra_deps=after)

        def batch_fix(inss):
            v = max(i.val for i in inss)
            for i in inss:
                i.val = v

        R_dmat, R_absd, R_mask, R_etmp = Res("dmat"), Res("absd"), Res("mask"), Res("etmp")

        S.add("pool", lambda e: e.memset(mhalf_sb, -0.5), writes=[R_mhalf])
        S.add("pool", lambda e: e.iota(dmat, [[-1, 128]], base=0, channel_multiplier=1,
                                       allow_small_or_imprecise_dtypes=True), writes=[R_dmat])
        tsc(ident, dmat, 0.0, None, ALU.is_equal, None, [R_dmat], [R_ident])
        R_negm = Res("negm")

        def gen_bias_pieces():
            pieces = []

            def prep(rel):
                S.add("pool", lambda e: e.iota(dmat, [[-1, 128]], base=rel * 128, channel_multiplier=1,
                                               allow_small_or_imprecise_dtypes=True), reads=[], writes=[R_dmat])
                act(absd, dmat, AF.Abs, [R_dmat], [R_absd])
                tsc(maskt, absd, 128.0, None, ALU.is_le, None, [R_absd], [R_mask])
                tsc(negm, maskt, 240000.0, -240000.0, ALU.mult, ALU.add, [R_mask], [R_negm])

            def one(rel, j, g):
                h = 4 * j + g
                slope = 2.0 ** (-(h + 1))
                idx = (rel + 1) * 2 + j
                stt(etmp, absd, -8.0 * slope, maskt, ALU.mult, ALU.mult, [R_absd, R_mask], [R_etmp])
                tt(etmp, etmp, negm, ALU.add, [R_etmp, R_negm], [R_etmp])
                S.add("dve", lambda e: e.tensor_copy(out=Bhi[:, idx, g * 128:(g + 1) * 128], in_=etmp),
                      reads=[R_etmp], writes=[R_E])

            for rel in (-1, 0, 1):
                for j in range(2):
                    for g in range(4):
                        def piece(rel=rel, j=j, g=g):
                            if j == 0 and g == 0:
                                prep(rel)
                            one(rel, j, g)
                        pieces.append(piece)
            return pieces

        def gen_bias():
            for pc in gen_bias_pieces():
                pc()

        dma("sp", valid_sb, valid_d, [], [R_valid], wstream("valid"))
        S.add("dve", lambda e: e.tensor_copy(out=vrep_sb, in_=valid_sb.unsqueeze(2).to_broadcast([128, TS + TP, 64])),
              reads=[R_valid], writes=[R_vrep])

        R_sst = [Res("ss%d" % t) for t in range(TP)]

        def stats_tile(t, junk, R_junk):
            S.add("dve", (lambda t: lambda e: e.memset(ss_sb[:, t:t + 1], 0.0))(t), writes=[R_sst[t]], cost=0.1)
            act(junk, x_sb[:, t, :], AF.Square, [R_x[t]], [R_junk, R_sst[t]], accum_out=ss_sb[:, t:t + 1])

        def norm_stats(tiles, junk, R_junk, compute=True):
            t0, t1 = tiles[0], tiles[-1] + 1
            if compute:
                for t in tiles:
                    stats_tile(t, junk, R_junk)
            tsc(ms_sb[:, t0:t1], ss_sb[:, t0:t1], 1.0 / D, EPS, ALU.mult, ALU.add,
                [R_sst[t] for t in tiles], [R_ms])
            tt(rstd_sb[:, t0:t1], ms_sb[:, t0:t1], mhalf_sb[:, t0:t1], ALU.pow, [R_ms, R_mhalf], [R_rstd], eng="pool")

        def make_hT(t, hb, R_hb, bpair, dst_fn, R_dst):
            tsc(hb, x_sb[:, t, :], rstd_sb[:, t:t + 1], None, ALU.mult, None, [R_x[t], R_rstd], [R_hb])
            for half in range(2):
                b = bpair[half]
                for kk in range(4):
                    kc = half * 4 + kk
                    mm(banks[b][:, kk * 128:(kk + 1) * 128], hb[:, kc * 128:(kc + 1) * 128], ident, True, True,
                       [R_hb, R_ident], [Rb[b]])
                if half == 0:
                    tt(dst_fn(half), banks[b].rearrange("p (a b) -> p a b", a=4),
                       gain_sb[:, half * 4:(half + 1) * 4].unsqueeze(2).to_broadcast([128, 4, 128]), ALU.mult,
                       [Rb[b], R_gain], [R_dst])
                else:
                    for kk in range(4):
                        kc = half * 4 + kk
                        act(dst_fn(half)[:, kk, :], banks[b][:, kk * 128:(kk + 1) * 128], AF.Copy,
                            [Rb[b], R_gain], [R_dst], scale=gain_sb[:, kc:kc + 1])

        FGS = [(0, 4), (4, 4), (8, 4), (12, 4), (16, 4), (20, 2)]

        def ffn_phase(l, which, tiles, fresh, hook=None, nobarrier=False):
            if not nobarrier:
                S.barrier()
            win_start = len(S.order)
            A = Alloc(PH_BASE)
            hT = A.get([128, 8, TP * 128], BF16)
            R_hT = [Res("hT%d" % t) for t in range(TP)]
            wg = [A.get([128, 8, 512], BF16) for _ in range(2)]
            wu = [A.get([128, 8, 512], BF16) for _ in range(2)]
            wd = [A.get([128, 4, D], BF16) for _ in range(2)]
            R_wg = [[Res("wg%d_%d" % (i, f_)) for f_ in range(4)] for i in range(2)]
            R_wu = [[Res("wu%d_%d" % (i, f_)) for f_ in range(4)] for i in range(2)]
            R_wd = [[Res("wd%d_%d" % (i, f_)) for f_ in range(4)] for i in range(2)]
            hb = [A.get([128, D], BF16) for _ in range(2)]
            R_hb = [Res("hb%d" % i) for i in range(2)]
            junk = A.get([128, D], BF16)
            R_junk = Res("junk")
            sg = [A.get([128, 256], F32) for _ in range(2)]
            R_sg = [Res("sg%d" % i) for i in range(2)]
            aT = [A.get([128, 256], BF16) for _ in range(4)]
            R_aT = [Res("aT%d" % i) for i in range(4)]
            assert A.off <= ARENA_BYTES - 4096, A.off

            dma("sp", gain_sb, gf_d[which][l], [], [R_gain], st_c)

            def load_w(k):
                slot = k % 2
                f0, nf = FGS[k]
                if k == 0:
                    prev = []
                    for fl in range(nf):
                        c0 = (f0 + fl) * 128
                        cur = [dma("pool", wg[slot][:, :, fl * 128:(fl + 1) * 128],
                                   wg_d[which][l][:, c0:c0 + 128].rearrange("(kc p) n -> p kc n", p=128),
                                   [], [R_wg[slot][fl]], wstream("wgf%d" % fl), after=prev),
                               dma("pool", wu[slot][:, :, fl * 128:(fl + 1) * 128],
                                   wu_d[which][l][:, c0:c0 + 128].rearrange("(kc p) n -> p kc n", p=128),
                                   [], [R_wu[slot][fl]], wstream("wuf%d" % fl), after=prev),
                               dma("pool", wd[slot][:, fl, :], wd_d[which][l][c0:c0 + 128, :],
                                   [], [R_wd[slot][fl]], wstream("wdf%d" % fl), after=prev)]
                        prev = cur
                    st["chain"] = prev
                    return
                after = st.pop("chain", []) if k == 1 else []
                dma("pool", wg[slot][:, :, 0:nf * 128],
                    wg_d[which][l][:, f0 * 128:(f0 + nf) * 128].rearrange("(kc p) n -> p kc n", p=128),
                    [], R_wg[slot][0:nf], wstream("wg%d" % slot), after=after)
                dma("pool", wu[slot][:, :, 0:nf * 128],
                    wu_d[which][l][:, f0 * 128:(f0 + nf) * 128].rearrange("(kc p) n -> p kc n", p=128),
                    [], R_wu[slot][0:nf], wstream("wu%d" % slot), after=after)
                dma("pool", wd[slot][:, 0:nf, :],
                    wd_d[which][l][f0 * 128:(f0 + nf) * 128, :].rearrange("(fc p) n -> p fc n", p=128),
                    [], R_wd[slot][0:nf], wstream("wd%d" % slot), after=after)

            st = {"k": 0, "slotctr": 0, "pend": None}
            load_w(0)
            load_w(1)
            norm_stats(tiles, junk, R_junk, compute=fresh)
            for idx, t in enumerate(tiles):
                bp = (0, 1) if idx % 2 == 0 else (2, 3)
                make_hT(t, hb[idx % 2], R_hb[idx % 2], bp,
                        (lambda t: lambda half: hT[:, half * 4:(half + 1) * 4, t * 128:(t + 1) * 128])(t), R_hT[t])

            groups = [tiles[i:i + 2] for i in range(0, len(tiles), 2)]

            def emit_GU(k, grp, fl):
                slot = k % 2
                it = st["k"]
                st["k"] += 1
                b = it % 2
                ntok = 128 * len(grp)
                tok0 = grp[0] * 128
                rh = [R_hT[t] for t in grp]
                for kc in range(8):
                    mm(banks[b][:, 0:ntok], wg[slot][:, kc, fl * 128:(fl + 1) * 128], hT[:, kc, tok0:tok0 + ntok],
                       kc == 0, kc == 7, rh + [R_wg[slot][fl]], [Rb[b]])
                for kc in range(8):
                    mm(banks[b][:, 256:256 + ntok], wu[slot][:, kc, fl * 128:(fl + 1) * 128],
                       hT[:, kc, tok0:tok0 + ntok], kc == 0, kc == 7, rh + [R_wu[slot][fl]], [Rb[b]])
                act(sg[b][:, 0:ntok], banks[b][:, 0:ntok], AF.Silu, [Rb[b]], [R_sg[b]])
                r = it % 4
                tt(aT[r][:, 0:ntok], sg[b][:, 0:ntok], banks[b][:, 256:256 + ntok], ALU.mult,
                   [R_sg[b], Rb[b]], [R_aT[r]])
                return r

            def emit_D(k, grp, fl, nf, r, accs):
                slot = k % 2
                for tl, t in enumerate(grp):
                    for half in range(2):
                        bk = accs[tl][half]
                        mm(banks[bk], aT[r][:, tl * 128:(tl + 1) * 128], wd[slot][:, fl, half * 512:(half + 1) * 512],
                           fl == 0, fl == nf - 1, [R_aT[r], R_wd[slot][fl]], [Rb[bk]])
                if fl == nf - 1:
                    for tl, t in enumerate(grp):
                        for half in range(2):
                            bk = accs[tl][half]
                            xv = x_sb[:, t, half * 512:(half + 1) * 512]
                            stt(xv, banks[bk], 0.5, xv, ALU.mult, ALU.add, [Rb[bk], R_x[t]], [R_x[t]])
                        if k == len(FGS) - 1:
                            stats_tile(t, junk, R_junk)

            pend_pieces = []
            for k, (f0, nf) in enumerate(FGS):
                for grp in groups:
                    if pend_pieces:
                        pend_pieces.pop(0)()
                    accs = []
                    for _ in grp:
                        s = st["slotctr"] % 3
                        st["slotctr"] += 1
                        accs.append((2 + 2 * s, 3 + 2 * s))
                    for fl in range(nf):
                        r = emit_GU(k, grp, fl)
                        if st["pend"] is not None:
                            emit_D(*st["pend"])
                        st["pend"] = (k, grp, fl, nf, r, accs)
                if k == 0 and cfg.get("ffnsched", True):
                    S.schedule_window(win_start)
                if k == 0 and hook is not None:
                    pend_pieces.extend(hook())
                if k + 2 < len(FGS):
                    if st["pend"] is not None:
                        emit_D(*st["pend"])
                        st["pend"] = None
                    load_w(k + 2)
            if st["pend"] is not None:
                emit_D(*st["pend"])
                st["pend"] = None
            while pend_pieces:
                pend_pieces.pop(0)()

        def mix_phase(l, T, vbase, kv_tiles, out_tiles):
            S.barrier()
            win_start = len(S.order)
            A = Alloc(PH_BASE)
            win = A.get([128, 8, WIN_EXT], BF16)
            R_win = Res("win")
            wout = A.get([128, 8, D], BF16)
            R_wout = Res("wout")
            wst = A.get([128, 8, 128], BF16)
            R_wst = Res("wst")
            bh = A.get([128, 512], F32)
            R_bh = Res("bh")
            vgbc = A.get([128, 512], F32)
            R_vg = Res("vg")
            sraw = A.get([128, 8], F32)
            R_sraw = Res("sraw")
            sexp = A.get([128, 2, 4, 128], F32)
            R_sexp = Res("sexp")
            kT = A.get([128, 2, TP * 128], BF16)
            R_kT = [Res("kT%d" % t) for t in range(TP)]
            vp = A.get([128, TP, 128], BF16)
            R_vp = [Res("vp%d" % t) for t in range(TP)]
            hb0 = A.get([128, D], BF16)
            R_hb0 = Res("hb0")
            hTt = [A.get([128, 8, 128], BF16) for _ in range(2)]
            R_hTt = [Res("hTt%d" % i) for i in range(2)]
            qTe = [A.get([128, 4, 128], BF16) for _ in range(2)]
            qTo = [A.get([128, 4, 128], BF16) for _ in range(2)]
            R_qT = [Res("qT%d" % i) for i in range(2)]
            tmpu0 = A.get([128, 512], F32)
            R_tmpu0 = Res("tmpu")
            tmpu_ = [tmpu0, tmpu0]
            R_tmpu_ = [R_tmpu0, R_tmpu0]
            tmpg_ = [A.get([128, 512], F32) for _ in range(2)]
            R_tmpg_ = [Res("tmpg%d" % i) for i in range(2)]
            tmpu, R_tmpu = tmpu_[0], R_tmpu_[0]
            gu = [A.get([128, 512], BF16) for _ in range(2)]
            R_gu = [Res("gu%d" % i) for i in range(2)]
            gg_ = [A.get([128, 512], F32) for _ in range(2)]
            R_gg_ = [Res("gg%d" % i) for i in range(2)]
            ssh = A.get([128, 8], F32)
            R_ssh = Res("ssh")
            r8 = A.get([128, 8], F32)
            R_r8 = Res("r8")
            ghn = [A.get([128, 512], BF16) for _ in range(2)]
            R_ghn = [Res("ghn%d" % i) for i in range(2)]
            PT = [A.get([128, 3, 512], BF16) for _ in range(2)]
            R_PT = [[Res("PT%d_%d" % (i, s_)) for s_ in range(3)] for i in range(2)]
            rden = A.get([128, 512], F32)
            R_rden = Res("rden")
            aTt = [A.get([128, 4, 128], BF16) for _ in range(2)]
            R_aTt = [Res("aTt%d" % i) for i in range(2)]
            mb = A.get([128, 512], F32)
            R_mb = Res("mb")
            numsb = A.get([128, 512], F32)
            R_numsb = Res("numsb")
            sexp2 = A.get([128, 4, 128], F32)
            R_sexp2 = Res("sexp2")
            cT = [A.get([128, 4, 128], BF16) for _ in range(2)]
            R_cT = [Res("cT%d" % i) for i in range(2)]
            junk = A.get([128, D], BF16)
            R_junk = Res("junkm")
            if cfg.get("verbose"):
                print("mix arena end", A.off, "of", ARENA_BYTES)

            R_wkv, R_wq, R_wu, R_wg = Res("win_kv"), Res("win_q"), Res("win_u"), Res("win_g")
            prev = []
            for nm, c0, c1, rr_ in (("kv", O_K, O_U, R_wkv), ("q", 0, O_K, R_wq), ("u", O_U, O_G, R_wu),
                                    ("g", O_G, WIN_EXT, R_wg)):
                prev = [dma("pool", win[:, :, c0:c1], win_d[l][:, c0:c1].rearrange("(kc p) n -> p kc n", p=128),
                            [], [rr_], wstream("win_" + nm), after=prev)]
            dwst = dma("pool", wst, wst_d[l], [], [R_wst], wstream("wst"), after=prev)
            ws2 = wstream("wout")
            b2 = [dma("pool", wout[:, :, c0:c0 + 512],
                      wout_d[l][:, c0:c0 + 512].rearrange("(kc p) n -> p kc n", p=128), [], [R_wout], ws2,
                      after=prev)
                  for c0 in (0, 512)]
            batch_fix(b2)
            b3 = [dma("sp", gain_sb, gm_d[l], [], [R_gain], st_c),
                  dma("sp", bh, bb_d[l], [], [R_bh], st_c),
                  dma("sp", vgbc, vg_d[l:l + 1, :].partition_broadcast(128), [], [R_vg], st_c),
                  dma("sp", sraw, sink_d[l:l + 1, :].partition_broadcast(128), [], [R_sraw], st_c)]
            batch_fix(b3)
            act(sraw, sraw, AF.Exp, [R_sraw], [R_sraw])
            for j in range(2):
                S.add("dve", (lambda j: lambda e: e.tensor_copy(
                    out=sexp[:, j, :, :], in_=sraw[:, j * 4:(j + 1) * 4].unsqueeze(2).to_broadcast([128, 4, 128])))(j),
                    reads=[R_sraw], writes=[R_sexp])
            for j in range(2):
                S.add("dve", (lambda j: lambda e: e.tensor_copy(
                    out=sexp2[j * 64:(j + 1) * 64, :, :],
                    in_=sraw[j * 64:(j + 1) * 64, j * 4:(j + 1) * 4].unsqueeze(2).to_broadcast([64, 4, 128])))(j),
                    reads=[R_sraw], writes=[R_sexp2])
            norm_stats(kv_tiles, junk, R_junk, compute=False)
            for pp in range(2):
                S.add("dve", (lambda pp: lambda e: e.memset(qTe[pp], 0.0))(pp), writes=[R_qT[pp]])
                S.add("dve", (lambda pp: lambda e: e.memset(qTo[pp], 0.0))(pp), writes=[R_qT[pp]])

            seqno = {"n": 0}
            last_touch = [0] * 8

            reserved = set()

            def nb():
                b = min((i for i in range(8) if i not in reserved), key=lambda i: last_touch[i])
                return b

            def touch(b):
                seqno["n"] += 1
                last_touch[b] = seqno["n"]

            sq_scale = float(np.sqrt(GK * GC))
            stt_ = {}

            def step_h(t):
                S.tag = "h(%d)" % t
                tsc(hb0, x_sb[:, t, :], rstd_sb[:, t:t + 1], None, ALU.mult, None, [R_x[t], R_rstd], [R_hb0])

            def step_T(t):
                S.tag = "T(%d)" % t
                bp = []
                for half in range(2):
                    b = nb()
                    touch(b)
                    bp.append(b)
                    for kk in range(4):
                        kc = half * 4 + kk
                        mm(banks[b][:, kk * 128:(kk + 1) * 128], hb0[:, kc * 128:(kc + 1) * 128], ident, True, True,
                           [R_hb0, R_ident], [Rb[b]])
                stt_[("T", t)] = bp

            def step_hTevac(t):
                S.tag = "hTevac(%d)" % t
                p = t % 2
                for half in range(2):
                    b = stt_[("T", t)][half]
                    tt(hTt[p][:, half * 4:(half + 1) * 4, :], banks[b].rearrange("p (a b) -> p a b", a=4),
                       gain_sb[:, half * 4:(half + 1) * 4].unsqueeze(2).to_broadcast([128, 4, 128]), ALU.mult,
                       [Rb[b], R_gain], [R_hTt[p]])
                    touch(b)

            def step_KV(t):
                S.tag = "KV(%d)" % t
                p = t % 2
                h = hTt[p]
                rh = [R_hTt[p], R_wkv]
                bk = nb()
                touch(bk)
                for j in range(2):
                    for kc in range(8):
                        mm(banks[bk][:, j * 128:(j + 1) * 128], win[:, kc, O_K + j * 128:O_K + (j + 1) * 128],
                           h[:, kc, :], kc == 0, kc == 7, rh, [Rb[bk]])
                for kc in range(8):
                    mm(banks[bk][:, 256:384], h[:, kc, :], win[:, kc, O_V:O_V + 128], kc == 0, kc == 7, rh, [Rb[bk]])
                act(kT[:, :, t * 128:(t + 1) * 128], banks[bk][:, 0:256].rearrange("p (a b) -> p a b", a=2),
                    AF.Copy, [Rb[bk]], [R_kT[t]])
                act(vp[:, t, :], banks[bk][:, 256:384], AF.Copy, [Rb[bk], R_valid], [R_vp[t]],
                    scale=valid_sb[:, vbase + t:vbase + t + 1])
                touch(bk)

            def step_QUG(t):
                S.tag = "QUG(%d)" % t
                p = t % 2
                h = hTt[p]
                rh = [R_hTt[p], R_wq]
                bq = nb()
                touch(bq)
                for c in range(4):
                    for kc in range(8):
                        mm(banks[bq][:, c * 128:(c + 1) * 128], win[:, kc, c * 128:(c + 1) * 128], h[:, kc, :],
                           kc == 0, kc == 7, rh, [Rb[bq]])
                bqv = banks[bq].rearrange("p (a b) -> p a b", a=4)
                act(qTe[p][0:64, :, :], bqv[0:64, :, :], AF.Copy, [Rb[bq]], [R_qT[p]])
                act(qTo[p][64:128, :, :], bqv[64:128, :, :], AF.Copy, [Rb[bq]], [R_qT[p]])
                touch(bq)
                bu = nb()
                touch(bu)
                rhu = [R_hTt[p], R_wu]
                for c in range(4):
                    for kc in range(8):
                        mm(banks[bu][:, c * 128:(c + 1) * 128], win[:, kc, O_U + c * 128:O_U + (c + 1) * 128],
                           h[:, kc, :], kc == 0, kc == 7, rhu, [Rb[bu]])
                bg = nb()
                touch(bg)
                rhg = [R_hTt[p], R_wg]
                for kc in range(8):
                    mm(banks[bg], h[:, kc, :], win[:, kc, O_G:O_G + 512], kc == 0, kc == 7, rhg, [Rb[bg]])
                reserved.add(bu)
                reserved.add(bg)
                stt_[("UG", t)] = (bu, bg)

            def step_gelu(t):
                S.tag = "gelu(%d)" % t
                p = t % 2
                bu, bg = stt_[("UG", t)]
                tmpu, R_tmpu, tmpg, R_tmpg, gg, R_gg = tmpu_[p], R_tmpu_[p], tmpg_[p], R_tmpg_[p], gg_[p], R_gg_[p]
                act(tmpu, banks[bu], AF.Square, [Rb[bu]], [R_tmpu], scale=sq_scale)
                act(tmpg, banks[bg], AF.Square, [Rb[bg]], [R_tmpg], scale=sq_scale)
                stt(tmpu, tmpu, GK, banks[bu], ALU.add, ALU.mult, [R_tmpu, Rb[bu]], [R_tmpu])
                stt(tmpg, tmpg, GK, banks[bg], ALU.add, ALU.mult, [R_tmpg, Rb[bg]], [R_tmpg])
                act(tmpu, tmpu, AF.Exp, [R_tmpu], [R_tmpu], scale=-2.0)
                act(tmpg, tmpg, AF.Exp, [R_tmpg], [R_tmpg], scale=-2.0)
                act(tmpu, tmpu, AF.Ln, [R_tmpu], [R_tmpu], bias=1.0)
                act(tmpg, tmpg, AF.Ln, [R_tmpg], [R_tmpg], bias=1.0)
                act(tmpu, tmpu, AF.Exp, [R_tmpu], [R_tmpu], scale=-1.0)
                act(tmpg, tmpg, AF.Exp, [R_tmpg], [R_tmpg], scale=-1.0)
                tt(gu[p], tmpu, banks[bu], ALU.mult, [R_tmpu, Rb[bu]], [R_gu[p]])
                touch(bu)
                reserved.discard(bu)
                tt(gg, tmpg, banks[bg], ALU.mult, [R_tmpg, Rb[bg]], [R_gg])
                touch(bg)
                reserved.discard(bg)
                tt(tmpg, gg, gg, ALU.mult, [R_gg], [R_tmpg])
                S.add("dve", lambda e: e.tensor_reduce(out=ssh, in_=tmpg.rearrange("p (h d) -> p h d", h=8),
                                                       axis=AX.X, op=ALU.add), reads=[R_tmpg], writes=[R_ssh])
                tsc(ssh, ssh, 1.0 / 64, EPS, ALU.mult, ALU.add, [R_ssh], [R_ssh])
                tt(r8, ssh, mhalf_sb[:, 0:8], ALU.pow, [R_ssh, R_mhalf], [R_r8], eng="pool")

            def step_ghn(t):
                S.tag = "ghn(%d)" % t
                p = t % 2
                gg, R_gg = gg_[p], R_gg_[p]
                tt(gg.rearrange("p (h d) -> p h d", h=8), gg.rearrange("p (h d) -> p h d", h=8),
                   r8.unsqueeze(2).to_broadcast([128, 8, 64]), ALU.mult, [R_gg, R_r8], [R_gg])
                tt(ghn[p], gg, vgbc, ALU.mult, [R_gg, R_vg], [R_ghn[p]])

            def step_mix(i):
                S.tag = "mix(%d)" % i
                p = i % 2
                bm = nb()
                touch(bm)
                reserved.add(bm)
                for h in range(8):
                    c, ph = h // 2, h % 2
                    mm(banks[bm][ph * 64:(ph + 1) * 64, c * 128:(c + 1) * 128], ghn[p][:, h * 64:(h + 1) * 64],
                       wst[:, h, :], True, True, [R_ghn[p], R_wst], [Rb[bm]])
                stt_[("M", i)] = bm

            def step_gmlp_elt(i):
                S.tag = "gmlp_elt(%d)" % i
                p = i % 2
                bm = stt_[("M", i)]
                tt(mb, banks[bm], bh, ALU.add, [Rb[bm], R_bh], [R_mb])
                touch(bm)
                reserved.discard(bm)
                tt(cT[p].rearrange("p a b -> p (a b)"), mb, gu[p], ALU.mult, [R_mb, R_gu[p]], [R_cT[p]])

            def kts_of(i):
                return [kt for kt in (i - 1, i, i + 1) if 0 <= kt < T]

            def step_S(i):
                S.tag = "S(%d)" % i
                p = i % 2
                for j in range(2):
                    for s, kt in enumerate(kts_of(i)):
                        rel = kt - i
                        idx = (rel + 1) * 2 + j
                        bs = nb()
                        touch(bs)
                        mm(banks[bs], ident, Bhi[:, idx, :], True, False, [R_ident, R_E], [Rb[bs]])
                        for g in range(4):
                            c = 2 * j + g // 2
                            ph = g % 2
                            mm(banks[bs][:, g * 128:(g + 1) * 128], kT[:, j, kt * 128:(kt + 1) * 128],
                               (qTe if ph == 0 else qTo)[p][:, c, :], False, g == 3, [R_kT[kt], R_qT[p]], [Rb[bs]])
                        act(PT[j][:, s, :], banks[bs], AF.Exp, [Rb[bs]], [R_PT[j][s]], scale=0.125)
                        touch(bs)

            def step_PV(i):
                S.tag = "PV(%d)" % i
                p = i % 2
                kts = kts_of(i)
                bn = nb()
                touch(bn)
                bd = nb()
                touch(bd)
                for j in range(2):
                    for s, kt in enumerate(kts):
                        mm(banks[bn][j * 64:(j + 1) * 64, :], vp[:, kt, j * 64:(j + 1) * 64], PT[j][:, s, :],
                           s == 0, s == len(kts) - 1, [R_vp[kt], R_PT[j][s]], [Rb[bn]])
                    for s, kt in enumerate(kts):
                        mm(banks[bd][j * 64:(j + 1) * 64, :], vrep_sb[:, vbase + kt, :], PT[j][:, s, :],
                           s == 0, s == len(kts) - 1, [R_vrep, R_PT[j][s]], [Rb[bd]])
                act(numsb, banks[bn], AF.Copy, [Rb[bn]], [R_numsb])
                touch(bn)
                tt(rden, banks[bd], sexp2.rearrange("p a b -> p (a b)"), ALU.add, [Rb[bd], R_sexp2], [R_rden])
                touch(bd)
                act(rden, rden, AF.Ln, [R_rden], [R_rden])
                act(rden, rden, AF.Exp, [R_rden], [R_rden], scale=-1.0)

            def step_norm_finish(i):
                S.tag = "norm_finish(%d)" % i
                p = i % 2
                for j in range(2):
                    numv = numsb[j * 64:(j + 1) * 64, :].rearrange("p (c two q) -> p c two q", c=2, two=2)
                    rdv = rden[j * 64:(j + 1) * 64, :].rearrange("p (c two q) -> p c two q", c=2, two=2)
                    for ph in range(2):
                        tt(aTt[p][ph * 64:(ph + 1) * 64, 2 * j:2 * j + 2, :], numv[:, :, ph, :], rdv[:, :, ph, :],
                           ALU.mult, [R_numsb, R_rden], [R_aTt[p]])

            def step_Wout(i):
                S.tag = "Wout(%d)" % i
                p = i % 2
                for half in range(2):
                    bo = nb()
                    touch(bo)
                    for c in range(4):
                        mm(banks[bo], aTt[p][:, c, :], wout[:, c, half * 512:(half + 1) * 512], c == 0, False,
                           [R_aTt[p], R_wout], [Rb[bo]])
                    for c in range(4):
                        mm(banks[bo], cT[p][:, c, :], wout[:, 4 + c, half * 512:(half + 1) * 512], False, c == 3,
                           [R_cT[p], R_wout], [Rb[bo]])
                    xv = x_sb[:, i, half * 512:(half + 1) * 512]
                    tt(xv, xv, banks[bo], ALU.add, [R_x[i], Rb[bo]], [R_x[i]])
                    touch(bo)
                stats_tile(i, junk, R_junk)

            K = list(kv_tiles)
            outs = set(out_tiles)
            step_h(K[0])
            step_T(K[0])
            step_hTevac(K[0])
            if len(K) > 1:
                step_h(K[1])
            step_KV(K[0])
            for n in range(len(K) + 2):
                t1 = K[n] if n < len(K) else None
                t1n = K[n + 1] if n + 1 < len(K) else None
                tb = K[n - 1] if (1 <= n <= len(K) and K[n - 1] in outs) else None
                tw = K[n - 2] if (2 <= n <= len(K) + 1 and K[n - 2] in outs) else None
                order = cfg.get("mixorder", DEFAULT_MIXORDER)
                for tok in order:
                    if tok == "S" and tb is not None:
                        step_S(tb)
                    elif tok == "QUG" and t1 is not None and t1 in outs:
                        step_QUG(t1)
                    elif tok == "T" and t1n is not None:
                        step_T(t1n)
                    elif tok == "hTevac" and t1n is not None:
                        step_hTevac(t1n)
                    elif tok == "h" and t1n is not None and n + 2 < len(K):
                        step_h(K[n + 2])
                    elif tok == "KV" and t1n is not None:
                        step_KV(t1n)
                    elif tok == "normf" and tw is not None:
                        step_norm_finish(tw)
                    elif tok == "Wout" and tw is not None:
                        step_Wout(tw)
                    elif tok == "mix" and tb is not None:
                        step_mix(tb)
                    elif tok == "gelu" and t1 is not None and t1 in outs:
                        step_gelu(t1)
                    elif tok == "gmlp" and tb is not None:
                        step_gmlp_elt(tb)
                    elif tok == "PV" and tb is not None:
                        step_PV(tb)
                    elif tok == "ghn" and t1 is not None and t1 in outs:
                        step_ghn(t1)
            if cfg.get("listsched", True):
                S.schedule_window(win_start)

        def final_phase(own_tiles, y_d):
            S.barrier()
            A = Alloc(PH_BASE)
            gfin = A.get([128, D], F32)
            R_gfin = Res("gfin")
            junk = A.get([128, D], BF16)
            R_junk = Res("junk")
            yb = [A.get([128, D], F32) for _ in range(4)]
            R_yb = [Res("yb%d" % i) for i in range(4)]
            dma("sp", gfin, gfin_d.partition_broadcast(128), [], [R_gfin], st_c)
            norm_stats(own_tiles, junk, R_junk, compute=False)
            outs = []
            for n, t in enumerate(own_tiles):
                r = n % 4
                stt(yb[r], x_sb[:, t, :], rstd_sb[:, t:t + 1], gfin, ALU.mult, ALU.mult,
                    [R_x[t], R_rstd, R_gfin], [R_yb[r]])
                outs.append(dma("sp", y_d[n * 128:(n + 1) * 128, :], yb[r], [R_yb[r]], [], st_y[r]))
            return outs

        st_y = [S.stream("y%d" % i) for i in range(4)]

        def segment(x_d, T, vbase, plan, own_tiles, y_d):
            for gi, t0 in enumerate(range(0, T, 4)):
                dma("sp", x_sb[:, t0:t0 + 4, :],
                    x_d[t0 * 128:(t0 + 4) * 128, :].rearrange("(t p) d -> p t d", p=128),
                    [], [R_x[t] for t in range(t0, t0 + 4)], wstream("x%d" % gi))
            sg_name = 's' if T == TS else 'p'
            if PH is not None:
                for l in range(2):
                    f1, kv, ot, f2 = plan[l]
                    if (sg_name, l, 'f1') in PH:
                        ffn_phase(l, 0, f1, True, hook=(gen_bias_pieces if not st_bias["done"] else None))
                        st_bias["done"] = True
                    if (sg_name, l, 'mix') in PH:
                        if not st_bias["done"]:
                            gen_bias()
                            st_bias["done"] = True
                        S.barrier()
                        A0 = Alloc(PH_BASE)
                        jk = A0.get([128, D], BF16)
                        norm_stats(kv, jk, Res("jk"), compute=True)
                        mix_phase(l, T, vbase, kv, ot)
                    if (sg_name, l, 'f2') in PH:
                        ffn_phase(l, 1, f2, True)
                S.barrier()
                A0 = Alloc(PH_BASE)
                jk = A0.get([128, D], BF16)
                norm_stats(own_tiles, jk, Res("jk"), compute=True)
                return final_phase(own_tiles, y_d)
            R67 = Res("b67")
            for t in range(T):
                S.add("dve", (lambda t: lambda e: e.memset(ss_sb[:, t:t + 1], 0.0))(t), writes=[R_sst[t]], cost=0.1)
                act(b67, x_sb[:, t, :], AF.Square, [R_x[t], Rb[6], Rb[7]], [Rb[6], Rb[7], R_sst[t]],
                    accum_out=ss_sb[:, t:t + 1])
            for l in range(2):
                f1, kv, ot, f2 = plan[l]
                ffn_phase(l, 0, f1, False, hook=(gen_bias_pieces if not st_bias["done"] else None),
                          nobarrier=(l == 0 and not st_bias["done"]))
                st_bias["done"] = True
                mix_phase(l, T, vbase, kv, ot)
                ffn_phase(l, 1, f2, False)
            return final_phase(own_tiles, y_d)

        allS = list(range(TS))
        plan_s = [(allS, allS, allS, allS), (allS, allS, allS, allS)]
        plan_p = [(list(range(0, 20)), list(range(0, 20)), list(range(1, 19)), list(range(1, 19))),
                  (list(range(1, 19)), list(range(1, 19)), list(range(2, 18)), list(range(2, 18)))]
        st_bias = {"done": False}
        o1 = segment(xs_d, TS, 0, plan_s, allS, ys_d) if 's' in SEGS else []
        o2 = segment(xp_d, TP, TS, plan_p, list(range(2, 18)), yp_d) if 'p' in SEGS else []
        S.add("sp", None, extra_deps=o1[-4:] + o2[-4:])

        S.lower(lambda name: es.enter_context(nc.semaphore(name)))
        with nc.Block() as block:
            @block.tensor
            def _(e):
                S.emit("pe", e)

            @block.scalar
            def _(e):
                S.emit("act", e)

            @block.vector
            def _(e):
                S.emit("dve", e)

            @block.gpsimd
            def _(e):
                S.emit("pool", e)

            @block.sync
            def _(e):
                S.emit("sp", e)
    return nc, S


def _prep_inputs(x_prompt, x_sample, norm_ffn1, w1_gate, w1_up, w1_down, norm_mix, w_in, sink,
                 gmlp_v_gain, w_spatial, b_spatial, w_out, norm_ffn2, w2_gate, w2_up, w2_down, norm_final):
    f = lambda a: np.ascontiguousarray(np.asarray(a, dtype=np.float32))
    w_in = f(w_in)
    k0 = w_in[:, :, 512:576]
    k1 = w_in[:, :, 576:640]
    w_in_ext = np.concatenate([w_in[:, :, 0:512], k0, k0, k1, k1, w_in[:, :, 640:768],
                               w_in[:, :, 768:1280], w_in[:, :, 1280:1792]], axis=2)
    gl = lambda g: f(np.asarray(g, np.float32).reshape(2, 8, 128).transpose(0, 2, 1))
    w_sT = f(np.asarray(w_spatial, np.float32).transpose(0, 3, 1, 2))
    bs = np.asarray(b_spatial, np.float32).reshape(2, 4, 2, 128).transpose(0, 2, 1, 3)
    b_bc = f(np.repeat(bs, 64, axis=1).reshape(2, 128, 512))
    shared = {
        "w1_gate": f(w1_gate), "w1_up": f(w1_up), "w1_down": f(w1_down),
        "w2_gate": f(w2_gate), "w2_up": f(w2_up), "w2_down": f(w2_down),
        "g_ffn1": gl(norm_ffn1), "g_ffn2": gl(norm_ffn2), "g_mix": gl(norm_mix),
        "w_in_ext": f(w_in_ext), "w_out": f(w_out), "w_sT": w_sT, "b_bc": b_bc,
        "vgain": f(gmlp_v_gain), "sink": f(sink), "g_fin": f(np.asarray(norm_final, np.float32).reshape(1, D)),
    }
    xp = np.asarray(x_prompt, np.float32)
    xs = np.asarray(x_sample, np.float32)
    in_maps = []
    for c in range(8):
        b, qd = c // 4, c % 4
        lo = qd * 2048 - 256
        hi = (qd + 1) * 2048 + 256
        xpc = np.zeros((TP * 128, D), np.float32)
        vmask = np.zeros((TP * 128,), np.float32)
        a, e = max(lo, 0), min(hi, 8192)
        xpc[a - lo:e - lo] = xp[b, a:e]
        vmask[a - lo:e - lo] = 1.0
        valid = np.concatenate([np.ones((128, TS), np.float32), vmask.reshape(TP, 128).T], axis=1)
        m = dict(shared)
        m["xs"] = f(xs[c])
        m["xp"] = xpc
        m["valid"] = f(valid)
        in_maps.append(m)
    return in_maps


_CACHE = {}


def kernel(**inputs):
    in_maps = _prep_inputs(**inputs)
    if "nc" not in _CACHE:
        _CACHE["nc"] = build_program()[0]
    nc = _CACHE["nc"]
    res = run_bass_kernel_spmd(nc, in_maps, core_ids=list(range(8)))
    y_prompt = np.zeros((2, 8192, D), np.float32)
    y_sample = np.zeros((8, 2048, D), np.float32)
    for c in range(8):
        r = res.results[c]
        b, qd = c // 4, c % 4
        y_prompt[b, qd * 2048:(qd + 1) * 2048] = r["yp"]
        y_sample[c] = r["ys"]
    return (y_prompt, y_sample)
```

```python
import contextlib
import numpy as np
import concourse.bass as bass
import concourse.mybir as mybir
from concourse.bass_utils import run_bass_kernel_spmd

F32 = mybir.dt.float32
BF16 = mybir.dt.bfloat16
ALU = mybir.AluOpType
AF = mybir.ActivationFunctionType
AX = mybir.AxisListType

D = 1024
DFF = 2816
NF = DFF // 128
EPS = 1e-6
WIN_EXT = 1920
O_K, O_V, O_U, O_G = 512, 768, 896, 1408
TS, TP = 16, 20
GK = 0.7978845608028654
GC = 0.044715
EPOCH = 16000
DEFAULT_MIXORDER = ["T", "hTevac", "h", "S", "KV", "QUG", "normf", "Wout", "mix", "gelu", "gmlp", "PV", "ghn"]


class Res:
    __slots__ = ("name", "w", "rs", "track")

    def __init__(self, name, track=True):
        self.name = name
        self.w = None
        self.rs = []
        self.track = track


class Stream:
    __slots__ = ("name", "n", "sem")

    def __init__(self, name):
        self.name = name
        self.n = 0
        self.sem = None


class Ins:
    __slots__ = ("eng", "idx", "fn", "deps", "inc", "val", "stream", "waits", "cnt", "tag", "cost")


class Sched:
    ENG = ("pe", "act", "dve", "pool", "sp")

    def __init__(self):
        self.q = {e: [] for e in self.ENG}
        self.last_real = {e: None for e in self.ENG}
        self.last_dma = {}
        self.order = []
        self.tag = ""
        self.streams = []

    def stream(self, name):
        s = Stream(name)
        self.streams.append(s)
        return s

    def add(self, eng, fn, reads=(), writes=(), stream=None, extra_deps=(), cost=0.3):
        ins = Ins()
        ins.eng = eng
        ins.fn = fn
        ins.idx = len(self.q[eng])
        ins.inc = False
        ins.stream = stream
        ins.val = None
        ins.waits = []
        ins.tag = self.tag
        ins.cost = cost
        deps = set(d for d in extra_deps if d is not None)
        for r in reads:
            if r.w is not None:
                deps.add(r.w)
        for w in writes:
            if w.w is not None:
                deps.add(w.w)
            for x in w.rs:
                deps.add(x)
        ins.deps = deps
        for r in reads:
            if r.track:
                r.rs.append(ins)
        for w in writes:
            w.w = ins
            w.rs = []
        self.q[eng].append(ins)
        self.order.append(ins)
        if fn is not None:
            self.last_real[eng] = ins
        if stream is not None:
            stream.n += 1
            ins.val = 16 * stream.n
            self.last_dma[stream] = ins
        return ins

    def schedule_window(self, a, lat=0.2):
        win = self.order[a:]
        if not win:
            return
        inwin = set(win)
        succ = {i: [] for i in win}
        npred = {}
        for i in win:
            ps = [d for d in i.deps if d in inwin and d is not i]
            npred[i] = len(ps)
            for d in ps:
                succ[d].append(i)
        prio = {}
        for i in reversed(win):
            m = 0.0
            for sx in succ[i]:
                if prio[sx] > m:
                    m = prio[sx]
            prio[i] = m + i.cost
        ready = {e: [] for e in self.ENG}
        for i in win:
            if npred[i] == 0:
                ready[i.eng].append(i)
        fin = {}
        free = {e: 0.0 for e in self.ENG}
        newq = {e: [] for e in self.ENG}
        left = len(win)
        while left:
            best = None
            for e in self.ENG:
                cands = ready[e]
                if not cands:
                    continue
                bi = None
                bs = None
                for i in cands:
                    st = free[e]
                    for d in i.deps:
                        if d in fin:
                            t = fin[d] + (lat if d.eng != e else 0.0)
                            if t > st:
                                st = t
                    key = (st, -prio[i])
                    if bs is None or key < bs:
                        bs = key
                        bi = i
                if best is None or bs < best[0]:
                    best = (bs, bi)
            (st, _), i = best
            e = i.eng
            ready[e].remove(i)
            if i.fn is None:
                fin[i] = st
                free[e] = st
            elif i.stream is not None:
                fin[i] = st + i.cost
                free[e] = st + 0.1
            else:
                fin[i] = st + i.cost
                free[e] = fin[i]
            newq[e].append(i)
            left -= 1
            for sx in succ[i]:
                npred[sx] -= 1
                if npred[sx] == 0:
                    ready[sx.eng].append(sx)
        for e in self.ENG:
            n = len(newq[e])
            if n:
                base = len(self.q[e]) - n
                assert set(self.q[e][base:]) == set(newq[e])
                self.q[e][base:] = newq[e]
                for k, i in enumerate(newq[e]):
                    i.idx = base + k
                reals = [i for i in newq[e] if i.fn is not None]
                if reals:
                    self.last_real[e] = reals[-1]

    def barrier(self):
        lasts = [self.last_real[e] for e in self.ENG if self.last_real[e] is not None]
        lasts += list(self.last_dma.values())
        for e in self.ENG:
            self.add(e, None, extra_deps=lasts)

    def lower(self, sem_ctx):
        for e in self.ENG:
            waited_idx = {}
            waited_dma = {}
            for ins in self.q[e]:
                need = {}
                for d in ins.deps:
                    if d is ins:
                        continue
                    if d.stream is not None:
                        if d.stream is ins.stream:
                            continue
                        if waited_dma.get(d.stream, 0) >= d.val:
                            continue
                        k = ("s", d.stream)
                        if k not in need or need[k].val < d.val:
                            need[k] = d
                    else:
                        if d.eng == e and (e == "pe" or d.idx >= ins.idx):
                            continue
                        if waited_idx.get(d.eng, -1) >= d.idx:
                            continue
                        k = ("e", d.eng)
                        if k not in need or need[k].idx < d.idx:
                            need[k] = d
                for k, d in need.items():
                    if k[0] == "s":
                        waited_dma[d.stream] = d.val
                    else:
                        waited_idx[d.eng] = d.idx
                        d.inc = True
                    ins.waits.append(d)
        nep = {}
        for e in self.ENG:
            c = 0
            for ins in self.q[e]:
                if ins.stream is None and ins.inc:
                    ins.cnt = c
                    c += 1
            nep[e] = max(1, (c + EPOCH - 1) // EPOCH)
        self.esems = {e: [sem_ctx("sem_%s_%d" % (e, i)) for i in range(nep[e])] for e in self.ENG}
        for s in self.streams:
            s.sem = sem_ctx("dma_" + s.name)

    def emit(self, eng, engobj):
        for ins in self.q[eng]:
            for d in ins.waits:
                if d.stream is not None:
                    engobj.wait_ge(d.stream.sem, d.val)
                else:
                    engobj.wait_ge(self.esems[d.eng][d.cnt // EPOCH], d.cnt % EPOCH + 1)
            if ins.fn is None:
                continue
            r = ins.fn(engobj)
            if ins.stream is not None:
                r.then_inc(ins.stream.sem, 16)
            elif ins.inc:
                r.then_inc(self.esems[eng][ins.cnt // EPOCH], 1)


def build_program(cfg=None):
    cfg = cfg or {}
    PH = cfg.get('phases')
    SEGS = cfg.get('segs', 'sp')
    MIXLVL = cfg.get('mixlevel', 99)
    nc = bass.Bass("TRN2", target_bir_lowering=False)
    S = Sched()

    def din(name, shape):
        return nc.dram_tensor(name, list(shape), F32, kind="ExternalInput").ap()

    xs_d = din("xs", [TS * 128, D])
    xp_d = din("xp", [TP * 128, D])
    valid_d = din("valid", [128, TS + TP])
    wg_d = [din("w1_gate", [2, D, DFF]), din("w2_gate", [2, D, DFF])]
    wu_d = [din("w1_up", [2, D, DFF]), din("w2_up", [2, D, DFF])]
    wd_d = [din("w1_down", [2, DFF, D]), din("w2_down", [2, DFF, D])]
    gf_d = [din("g_ffn1", [2, 128, 8]), din("g_ffn2", [2, 128, 8])]
    gm_d = din("g_mix", [2, 128, 8])
    win_d = din("w_in_ext", [2, D, WIN_EXT])
    wout_d = din("w_out", [2, D, D])
    wst_d = din("w_sT", [2, 128, 8, 128])
    bb_d = din("b_bc", [2, 128, 512])
    vg_d = din("vgain", [2, 512])
    sink_d = din("sink", [2, 8])
    gfin_d = din("g_fin", [1, D])
    ys_d = nc.dram_tensor("ys", [TS * 128, D], F32, kind="ExternalOutput").ap()
    yp_d = nc.dram_tensor("yp", [16 * 128, D], F32, kind="ExternalOutput").ap()

    es = contextlib.ExitStack()
    with es:
        ARENA_BYTES = 207 * 1024
        arena = es.enter_context(nc.sbuf_tensor("arena", [128, ARENA_BYTES // 4], F32))
        banks = [es.enter_context(nc.psum_tensor("bank%d" % i, [128, 512], F32))[:] for i in range(6)]
        b67 = es.enter_context(nc.psum_tensor("bank67", [128, 1024], F32))[:]
        banks += [b67[:, 0:512], b67[:, 512:1024]]
        Rb = [Res("bank%d" % i) for i in range(8)]

        class Alloc:
            def __init__(self, base):
                self.off = base

            def get(self, shape, dt):
                n = 1
                for s in shape[1:]:
                    n *= s
                nbytes = n * (4 if dt == F32 else 2)
                nbytes = (nbytes + 63) // 64 * 64
                a = arena[:, self.off // 4:(self.off + nbytes) // 4]
                if dt != F32:
                    a = a.bitcast(dt)
                    a = a[:, 0:n]
                else:
                    a = a[:, 0:n]
                if len(shape) == 3:
                    a = a.rearrange("p (a b) -> p a b", a=shape[1])
                elif len(shape) == 4:
                    a = a.rearrange("p (a b c) -> p a b c", a=shape[1], b=shape[2])
                self.off += nbytes
                assert self.off <= ARENA_BYTES, "SBUF arena overflow %d" % self.off
                return a

        P = Alloc(0)
        x_sb = P.get([128, TP, D], F32)
        R_x = [Res("x%d" % t) for t in range(TP)]
        ident = P.get([128, 128], BF16)
        R_ident = Res("ident", track=False)
        Bhi = P.get([128, 6, 512], BF16)
        R_E = Res("E", track=False)
        valid_sb = P.get([128, TS + TP], F32)
        R_valid = Res("valid", track=False)
        vrep_sb = P.get([128, TS + TP, 64], BF16)
        R_vrep = Res("vrep", track=False)
        ss_sb = P.get([128, TP], F32)
        R_ss = Res("ss")
        ms_sb = P.get([128, TP], F32)
        R_ms = Res("ms")
        rstd_sb = P.get([128, TP], F32)
        R_rstd = Res("rstd")
        mhalf_sb = P.get([128, TP], F32)
        R_mhalf = Res("mhalf", track=False)
        gain_sb = P.get([128, 8], F32)
        R_gain = Res("gain")
        PH_BASE = P.off
        SCR = Alloc(ARENA_BYTES - 4096)
        dmat = SCR.get([128, 128], F32)
        absd = SCR.get([128, 128], F32)
        maskt = SCR.get([128, 128], F32)
        etmp = SCR.get([128, 128], F32)
        negm = SCR.get([128, 128], F32)

        st_x = S.stream("x")
        st_c = S.stream("const")
        st_w = {}

        def wstream(name):
            if name not in st_w:
                st_w[name] = S.stream(name)
            return st_w[name]

        def fsz(ap):
            n = 1
            for d in ap.shape[1:]:
                n *= d
            return n

        def mm(out, lhsT, rhs, start, stop, reads, writes):
            return S.add("pe", lambda e: e.matmul(out, lhsT=lhsT, rhs=rhs, start=start, stop=stop),
                         reads=reads, writes=writes, cost=max(fsz(out), 64) / 2400.0 + 0.003)

        def act(out, in_, func, reads, writes, scale=1.0, accum_out=None, bias=None):
            c = 0.2 + fsz(out) / 1200.0
            if bias is not None:
                return S.add("act", lambda e: e.activation(out=out, in_=in_, func=func, scale=scale, bias=bias),
                             reads=reads, writes=writes, cost=c)
            if accum_out is None:
                return S.add("act", lambda e: e.activation(out=out, in_=in_, func=func, scale=scale),
                             reads=reads, writes=writes, cost=c)
            return S.add("act", lambda e: e.activation(out=out, in_=in_, func=func, scale=scale,
                                                       accum_out=accum_out), reads=reads, writes=writes, cost=c)

        def tt(out, in0, in1, op, reads, writes, eng="dve"):
            c = (0.15 + fsz(out) / 960.0) if eng == "dve" else (0.4 + fsz(out) / 650.0)
            return S.add(eng, lambda e: e.tensor_tensor(out=out, in0=in0, in1=in1, op=op),
                         reads=reads, writes=writes, cost=c)

        def tsc(out, in0, s1, s2, op0, op1, reads, writes):
            c = 0.15 + fsz(out) / 1900.0
            if s2 is None:
                return S.add("dve", lambda e: e.tensor_scalar(out=out, in0=in0, scalar1=s1, scalar2=None, op0=op0),
                             reads=reads, writes=writes, cost=c)
            return S.add("dve", lambda e: e.tensor_scalar(out=out, in0=in0, scalar1=s1, scalar2=s2, op0=op0, op1=op1),
                         reads=reads, writes=writes, cost=c)

        def stt(out, in0, scalar, in1, op0, op1, reads, writes):
            return S.add("dve", lambda e: e.scalar_tensor_tensor(out=out, in0=in0, scalar=scalar, in1=in1,
                                                                 op0=op0, op1=op1), reads=reads, writes=writes,
                         cost=0.15 + fsz(out) / 960.0)

        def dma(eng, out, in_, reads, writes, stream):
            return S.add(eng, lambda e: e.dma_start(out=out, in_=in_), reads=reads, writes=writes, stream=stream,
                         cost=2.0 + 128 * fsz(out) * 4 / 250e3)

        def batch_fix(inss):
            v = max(i.val for i in inss)
            for i in inss:
                i.val = v

        R_dmat, R_absd, R_mask, R_etmp = Res("dmat"), Res("absd"), Res("mask"), Res("etmp")

        S.add("pool", lambda e: e.memset(mhalf_sb, -0.5), writes=[R_mhalf])
        S.add("pool", lambda e: e.iota(dmat, [[-1, 128]], base=0, channel_multiplier=1,
                                       allow_small_or_imprecise_dtypes=True), writes=[R_dmat])
        tsc(ident, dmat, 0.0, None, ALU.is_equal, None, [R_dmat], [R_ident])
        R_negm = Res("negm")

        def gen_bias_pieces():
            pieces = []

            def prep(rel):
                S.add("pool", lambda e: e.iota(dmat, [[-1, 128]], base=rel * 128, channel_multiplier=1,
                                               allow_small_or_imprecise_dtypes=True), reads=[], writes=[R_dmat])
                act(absd, dmat, AF.Abs, [R_dmat], [R_absd])
                tsc(maskt, absd, 128.0, None, ALU.is_le, None, [R_absd], [R_mask])
                tsc(negm, maskt, 240000.0, -240000.0, ALU.mult, ALU.add, [R_mask], [R_negm])

            def one(rel, j, g):
                h = 4 * j + g
                slope = 2.0 ** (-(h + 1))
                idx = (rel + 1) * 2 + j
                stt(etmp, absd, -8.0 * slope, maskt, ALU.mult, ALU.mult, [R_absd, R_mask], [R_etmp])
                tt(etmp, etmp, negm, ALU.add, [R_etmp, R_negm], [R_etmp])
                S.add("dve", lambda e: e.tensor_copy(out=Bhi[:, idx, g * 128:(g + 1) * 128], in_=etmp),
                      reads=[R_etmp], writes=[R_E])

            for rel in (-1, 0, 1):
                for j in range(2):
                    for g in range(4):
                        def piece(rel=rel, j=j, g=g):
                            if j == 0 and g == 0:
                                prep(rel)
                            one(rel, j, g)
                        pieces.append(piece)
            return pieces

        def gen_bias():
            for pc in gen_bias_pieces():
                pc()

        dma("sp", valid_sb, valid_d, [], [R_valid], wstream("valid"))
        S.add("dve", lambda e: e.tensor_copy(out=vrep_sb, in_=valid_sb.unsqueeze(2).to_broadcast([128, TS + TP, 64])),
              reads=[R_valid], writes=[R_vrep])

        R_sst = [Res("ss%d" % t) for t in range(TP)]

        def stats_tile(t, junk, R_junk):
            S.add("dve", (lambda t: lambda e: e.memset(ss_sb[:, t:t + 1], 0.0))(t), writes=[R_sst[t]], cost=0.1)
            act(junk, x_sb[:, t, :], AF.Square, [R_x[t]], [R_junk, R_sst[t]], accum_out=ss_sb[:, t:t + 1])

        def norm_stats(tiles, junk, R_junk, compute=True):
            t0, t1 = tiles[0], tiles[-1] + 1
            if compute:
                for t in tiles:
                    stats_tile(t, junk, R_junk)
            tsc(ms_sb[:, t0:t1], ss_sb[:, t0:t1], 1.0 / D, EPS, ALU.mult, ALU.add,
                [R_sst[t] for t in tiles], [R_ms])
            tt(rstd_sb[:, t0:t1], ms_sb[:, t0:t1], mhalf_sb[:, t0:t1], ALU.pow, [R_ms, R_mhalf], [R_rstd], eng="pool")

        def make_hT(t, hb, R_hb, bpair, dst_fn, R_dst):
            tsc(hb, x_sb[:, t, :], rstd_sb[:, t:t + 1], None, ALU.mult, None, [R_x[t], R_rstd], [R_hb])
            for half in range(2):
                b = bpair[half]
                for kk in range(4):
                    kc = half * 4 + kk
                    mm(banks[b][:, kk * 128:(kk + 1) * 128], hb[:, kc * 128:(kc + 1) * 128], ident, True, True,
                       [R_hb, R_ident], [Rb[b]])
                if half == 0:
                    tt(dst_fn(half), banks[b].rearrange("p (a b) -> p a b", a=4),
                       gain_sb[:, half * 4:(half + 1) * 4].unsqueeze(2).to_broadcast([128, 4, 128]), ALU.mult,
                       [Rb[b], R_gain], [R_dst])
                else:
                    for kk in range(4):
                        kc = half * 4 + kk
                        act(dst_fn(half)[:, kk, :], banks[b][:, kk * 128:(kk + 1) * 128], AF.Copy,
                            [Rb[b], R_gain], [R_dst], scale=gain_sb[:, kc:kc + 1])

        FGS = [(0, 4), (4, 4), (8, 4), (12, 4), (16, 4), (20, 2)]

        def ffn_phase(l, which, tiles, fresh, hook=None, nobarrier=False):
            if not nobarrier:
                S.barrier()
            win_start = len(S.order)
            A = Alloc(PH_BASE)
            hT = A.get([128, 8, TP * 128], BF16)
            R_hT = [Res("hT%d" % t) for t in range(TP)]
            wg = [A.get([128, 8, 512], BF16) for _ in range(2)]
            wu = [A.get([128, 8, 512], BF16) for _ in range(2)]
            wd = [A.get([128, 4, D], BF16) for _ in range(2)]
            R_wg = [[Res("wg%d_%d" % (i, f_)) for f_ in range(4)] for i in range(2)]
            R_wu = [[Res("wu%d_%d" % (i, f_)) for f_ in range(4)] for i in range(2)]
            R_wd = [[Res("wd%d_%d" % (i, f_)) for f_ in range(4)] for i in range(2)]
            hb = [A.get([128, D], BF16) for _ in range(2)]
            R_hb = [Res("hb%d" % i) for i in range(2)]
            junk = A.get([128, D], BF16)
            R_junk = Res("junk")
            sg = [A.get([128, 256], F32) for _ in range(2)]
            R_sg = [Res("sg%d" % i) for i in range(2)]
            aT = [A.get([128, 256], BF16) for _ in range(4)]
            R_aT = [Res("aT%d" % i) for i in range(4)]
            assert A.off <= ARENA_BYTES - 4096, A.off

            dma("sp", gain_sb, gf_d[which][l], [], [R_gain], st_c)

            def load_w(k):
                slot = k % 2
                f0, nf = FGS[k]
                if k == 0:
                    for fl in range(nf):
                        c0 = (f0 + fl) * 128
                        dma("pool", wg[slot][:, :, fl * 128:(fl + 1) * 128],
                            wg_d[which][l][:, c0:c0 + 128].rearrange("(kc p) n -> p kc n", p=128),
                            [], [R_wg[slot][fl]], wstream("wgf%d" % fl))
                        dma("pool", wu[slot][:, :, fl * 128:(fl + 1) * 128],
                            wu_d[which][l][:, c0:c0 + 128].rearrange("(kc p) n -> p kc n", p=128),
                            [], [R_wu[slot][fl]], wstream("wuf%d" % fl))
                        dma("pool", wd[slot][:, fl, :], wd_d[which][l][c0:c0 + 128, :],
                            [], [R_wd[slot][fl]], wstream("wdf%d" % fl))
                    return
                dma("pool", wg[slot][:, :, 0:nf * 128],
                    wg_d[which][l][:, f0 * 128:(f0 + nf) * 128].rearrange("(kc p) n -> p kc n", p=128),
                    [], R_wg[slot][0:nf], wstream("wg%d" % slot))
                dma("pool", wu[slot][:, :, 0:nf * 128],
                    wu_d[which][l][:, f0 * 128:(f0 + nf) * 128].rearrange("(kc p) n -> p kc n", p=128),
                    [], R_wu[slot][0:nf], wstream("wu%d" % slot))
                dma("pool", wd[slot][:, 0:nf, :],
                    wd_d[which][l][f0 * 128:(f0 + nf) * 128, :].rearrange("(fc p) n -> p fc n", p=128),
                    [], R_wd[slot][0:nf], wstream("wd%d" % slot))

            load_w(0)
            load_w(1)
            norm_stats(tiles, junk, R_junk, compute=fresh)
            for idx, t in enumerate(tiles):
                bp = (0, 1) if idx % 2 == 0 else (2, 3)
                make_hT(t, hb[idx % 2], R_hb[idx % 2], bp,
                        (lambda t: lambda half: hT[:, half * 4:(half + 1) * 4, t * 128:(t + 1) * 128])(t), R_hT[t])

            groups = [tiles[i:i + 2] for i in range(0, len(tiles), 2)]
            st = {"k": 0, "slotctr": 0, "pend": None}

            def emit_GU(k, grp, fl):
                slot = k % 2
                it = st["k"]
                st["k"] += 1
                b = it % 2
                ntok = 128 * len(grp)
                tok0 = grp[0] * 128
                rh = [R_hT[t] for t in grp]
                for kc in range(8):
                    mm(banks[b][:, 0:ntok], wg[slot][:, kc, fl * 128:(fl + 1) * 128], hT[:, kc, tok0:tok0 + ntok],
                       kc == 0, kc == 7, rh + [R_wg[slot][fl]], [Rb[b]])
                for kc in range(8):
                    mm(banks[b][:, 256:256 + ntok], wu[slot][:, kc, fl * 128:(fl + 1) * 128],
                       hT[:, kc, tok0:tok0 + ntok], kc == 0, kc == 7, rh + [R_wu[slot][fl]], [Rb[b]])
                act(sg[b][:, 0:ntok], banks[b][:, 0:ntok], AF.Silu, [Rb[b]], [R_sg[b]])
                r = it % 4
                tt(aT[r][:, 0:ntok], sg[b][:, 0:ntok], banks[b][:, 256:256 + ntok], ALU.mult,
                   [R_sg[b], Rb[b]], [R_aT[r]])
                return r

            def emit_D(k, grp, fl, nf, r, accs):
                slot = k % 2
                for tl, t in enumerate(grp):
                    for half in range(2):
                        bk = accs[tl][half]
                        mm(banks[bk], aT[r][:, tl * 128:(tl + 1) * 128], wd[slot][:, fl, half * 512:(half + 1) * 512],
                           fl == 0, fl == nf - 1, [R_aT[r], R_wd[slot][fl]], [Rb[bk]])
                if fl == nf - 1:
                    for tl, t in enumerate(grp):
                        for half in range(2):
                            bk = accs[tl][half]
                            xv = x_sb[:, t, half * 512:(half + 1) * 512]
                            stt(xv, banks[bk], 0.5, xv, ALU.mult, ALU.add, [Rb[bk], R_x[t]], [R_x[t]])
                        if k == len(FGS) - 1:
                            stats_tile(t, junk, R_junk)

            pend_pieces = []
            for k, (f0, nf) in enumerate(FGS):
                for grp in groups:
                    if pend_pieces:
                        pend_pieces.pop(0)()
                    accs = []
                    for _ in grp:
                        s = st["slotctr"] % 3
                        st["slotctr"] += 1
                        accs.append((2 + 2 * s, 3 + 2 * s))
                    for fl in range(nf):
                        r = emit_GU(k, grp, fl)
                        if st["pend"] is not None:
                            emit_D(*st["pend"])
                        st["pend"] = (k, grp, fl, nf, r, accs)
                if k == 0 and cfg.get("ffnsched", True):
                    S.schedule_window(win_start)
                if k == 0 and hook is not None:
                    pend_pieces.extend(hook())
                if k + 2 < len(FGS):
                    if st["pend"] is not None:
                        emit_D(*st["pend"])
                        st["pend"] = None
                    load_w(k + 2)
            if st["pend"] is not None:
                emit_D(*st["pend"])
                st["pend"] = None
            while pend_pieces:
                pend_pieces.pop(0)()

        def mix_phase(l, T, vbase, kv_tiles, out_tiles):
            S.barrier()
            win_start = len(S.order)
            A = Alloc(PH_BASE)
            win = A.get([128, 8, WIN_EXT], BF16)
            R_win = Res("win")
            wout = A.get([128, 8, D], BF16)
            R_wout = Res("wout")
            wst = A.get([128, 8, 128], BF16)
            R_wst = Res("wst")
            bh = A.get([128, 512], F32)
            R_bh = Res("bh")
            vgbc = A.get([128, 512], F32)
            R_vg = Res("vg")
            sraw = A.get([128, 8], F32)
            R_sraw = Res("sraw")
            sexp = A.get([128, 2, 4, 128], F32)
            R_sexp = Res("sexp")
            kT = A.get([128, 2, TP * 128], BF16)
            R_kT = [Res("kT%d" % t) for t in range(TP)]
            vp = A.get([128, TP, 128], BF16)
            R_vp = [Res("vp%d" % t) for t in range(TP)]
            hb0 = A.get([128, D], BF16)
            R_hb0 = Res("hb0")
            hTt = [A.get([128, 8, 128], BF16) for _ in range(2)]
            R_hTt = [Res("hTt%d" % i) for i in range(2)]
            qTe = [A.get([128, 4, 128], BF16) for _ in range(2)]
            qTo = [A.get([128, 4, 128], BF16) for _ in range(2)]
            R_qT = [Res("qT%d" % i) for i in range(2)]
            tmpu0 = A.get([128, 512], F32)
            R_tmpu0 = Res("tmpu")
            tmpu_ = [tmpu0, tmpu0]
            R_tmpu_ = [R_tmpu0, R_tmpu0]
            tmpg_ = [A.get([128, 512], F32) for _ in range(2)]
            R_tmpg_ = [Res("tmpg%d" % i) for i in range(2)]
            tmpu, R_tmpu = tmpu_[0], R_tmpu_[0]
            gu = [A.get([128, 512], BF16) for _ in range(2)]
            R_gu = [Res("gu%d" % i) for i in range(2)]
            gg_ = [A.get([128, 512], F32) for _ in range(2)]
            R_gg_ = [Res("gg%d" % i) for i in range(2)]
            ssh = A.get([128, 8], F32)
            R_ssh = Res("ssh")
            r8 = A.get([128, 8], F32)
            R_r8 = Res("r8")
            ghn = [A.get([128, 512], BF16) for _ in range(2)]
            R_ghn = [Res("ghn%d" % i) for i in range(2)]
            PT = [A.get([128, 3, 512], BF16) for _ in range(2)]
            R_PT = [[Res("PT%d_%d" % (i, s_)) for s_ in range(3)] for i in range(2)]
            rden = A.get([128, 512], F32)
            R_rden = Res("rden")
            aTt = [A.get([128, 4, 128], BF16) for _ in range(2)]
            R_aTt = [Res("aTt%d" % i) for i in range(2)]
            mb = A.get([128, 512], F32)
            R_mb = Res("mb")
            numsb = A.get([128, 512], F32)
            R_numsb = Res("numsb")
            sexp2 = A.get([128, 4, 128], F32)
            R_sexp2 = Res("sexp2")
            cT = [A.get([128, 4, 128], BF16) for _ in range(2)]
            R_cT = [Res("cT%d" % i) for i in range(2)]
            junk = A.get([128, D], BF16)
            R_junk = Res("junkm")
            if cfg.get("verbose"):
                print("mix arena end", A.off, "of", ARENA_BYTES)

            R_wkv, R_wq, R_wu, R_wg = Res("win_kv"), Res("win_q"), Res("win_u"), Res("win_g")
            for nm, c0, c1, rr_ in (("kv", O_K, O_U, R_wkv), ("q", 0, O_K, R_wq), ("u", O_U, O_G, R_wu),
                                    ("g", O_G, WIN_EXT, R_wg)):
                dma("pool", win[:, :, c0:c1], win_d[l][:, c0:c1].rearrange("(kc p) n -> p kc n", p=128),
                    [], [rr_], wstream("win_" + nm))
            ws2 = wstream("wout")
            b2 = [dma("pool", wout[:, :, c0:c0 + 512],
                      wout_d[l][:, c0:c0 + 512].rearrange("(kc p) n -> p kc n", p=128), [], [R_wout], ws2)
                  for c0 in (0, 512)]
            batch_fix(b2)
            dma("pool", wst, wst_d[l], [], [R_wst], wstream("wst"))
            b3 = [dma("sp", gain_sb, gm_d[l], [], [R_gain], st_c),
                  dma("sp", bh, bb_d[l], [], [R_bh], st_c),
                  dma("sp", vgbc, vg_d[l:l + 1, :].partition_broadcast(128), [], [R_vg], st_c),
                  dma("sp", sraw, sink_d[l:l + 1, :].partition_broadcast(128), [], [R_sraw], st_c)]
            batch_fix(b3)
            act(sraw, sraw, AF.Exp, [R_sraw], [R_sraw])
            for j in range(2):
                S.add("dve", (lambda j: lambda e: e.tensor_copy(
                    out=sexp[:, j, :, :], in_=sraw[:, j * 4:(j + 1) * 4].unsqueeze(2).to_broadcast([128, 4, 128])))(j),
                    reads=[R_sraw], writes=[R_sexp])
            for j in range(2):
                S.add("dve", (lambda j: lambda e: e.tensor_copy(
                    out=sexp2[j * 64:(j + 1) * 64, :, :],
                    in_=sraw[j * 64:(j + 1) * 64, j * 4:(j + 1) * 4].unsqueeze(2).to_broadcast([64, 4, 128])))(j),
                    reads=[R_sraw], writes=[R_sexp2])
            norm_stats(kv_tiles, junk, R_junk, compute=False)
            for pp in range(2):
                S.add("dve", (lambda pp: lambda e: e.memset(qTe[pp], 0.0))(pp), writes=[R_qT[pp]])
                S.add("dve", (lambda pp: lambda e: e.memset(qTo[pp], 0.0))(pp), writes=[R_qT[pp]])

            seqno = {"n": 0}
            last_touch = [0] * 8

            reserved = set()

            def nb():
                b = min((i for i in range(8) if i not in reserved), key=lambda i: last_touch[i])
                return b

            def touch(b):
                seqno["n"] += 1
                last_touch[b] = seqno["n"]

            sq_scale = float(np.sqrt(GK * GC))
            stt_ = {}

            def step_h(t):
                S.tag = "h(%d)" % t
                tsc(hb0, x_sb[:, t, :], rstd_sb[:, t:t + 1], None, ALU.mult, None, [R_x[t], R_rstd], [R_hb0])

            def step_T(t):
                S.tag = "T(%d)" % t
                bp = []
                for half in range(2):
                    b = nb()
                    touch(b)
                    bp.append(b)
                    for kk in range(4):
                        kc = half * 4 + kk
                        mm(banks[b][:, kk * 128:(kk + 1) * 128], hb0[:, kc * 128:(kc + 1) * 128], ident, True, True,
                           [R_hb0, R_ident], [Rb[b]])
                stt_[("T", t)] = bp

            def step_hTevac(t):
                S.tag = "hTevac(%d)" % t
                p = t % 2
                for half in range(2):
                    b = stt_[("T", t)][half]
                    tt(hTt[p][:, half * 4:(half + 1) * 4, :], banks[b].rearrange("p (a b) -> p a b", a=4),
                       gain_sb[:, half * 4:(half + 1) * 4].unsqueeze(2).to_broadcast([128, 4, 128]), ALU.mult,
                       [Rb[b], R_gain], [R_hTt[p]])
                    touch(b)

            def step_KV(t):
                S.tag = "KV(%d)" % t
                p = t % 2
                h = hTt[p]
                rh = [R_hTt[p], R_wkv]
                bk = nb()
                touch(bk)
                for j in range(2):
                    for kc in range(8):
                        mm(banks[bk][:, j * 128:(j + 1) * 128], win[:, kc, O_K + j * 128:O_K + (j + 1) * 128],
                           h[:, kc, :], kc == 0, kc == 7, rh, [Rb[bk]])
                for kc in range(8):
                    mm(banks[bk][:, 256:384], h[:, kc, :], win[:, kc, O_V:O_V + 128], kc == 0, kc == 7, rh, [Rb[bk]])
                act(kT[:, :, t * 128:(t + 1) * 128], banks[bk][:, 0:256].rearrange("p (a b) -> p a b", a=2),
                    AF.Copy, [Rb[bk]], [R_kT[t]])
                act(vp[:, t, :], banks[bk][:, 256:384], AF.Copy, [Rb[bk], R_valid], [R_vp[t]],
                    scale=valid_sb[:, vbase + t:vbase + t + 1])
                touch(bk)

            def step_QUG(t):
                S.tag = "QUG(%d)" % t
                p = t % 2
                h = hTt[p]
                rh = [R_hTt[p], R_wq]
                bq = nb()
                touch(bq)
                for c in range(4):
                    for kc in range(8):
                        mm(banks[bq][:, c * 128:(c + 1) * 128], win[:, kc, c * 128:(c + 1) * 128], h[:, kc, :],
                           kc == 0, kc == 7, rh, [Rb[bq]])
                bqv = banks[bq].rearrange("p (a b) -> p a b", a=4)
                act(qTe[p][0:64, :, :], bqv[0:64, :, :], AF.Copy, [Rb[bq]], [R_qT[p]])
                act(qTo[p][64:128, :, :], bqv[64:128, :, :], AF.Copy, [Rb[bq]], [R_qT[p]])
                touch(bq)
                bu = nb()
                touch(bu)
                rhu = [R_hTt[p], R_wu]
                for c in range(4):
                    for kc in range(8):
                        mm(banks[bu][:, c * 128:(c + 1) * 128], win[:, kc, O_U + c * 128:O_U + (c + 1) * 128],
                           h[:, kc, :], kc == 0, kc == 7, rhu, [Rb[bu]])
                bg = nb()
                touch(bg)
                rhg = [R_hTt[p], R_wg]
                for kc in range(8):
                    mm(banks[bg], h[:, kc, :], win[:, kc, O_G:O_G + 512], kc == 0, kc == 7, rhg, [Rb[bg]])
                reserved.add(bu)
                reserved.add(bg)
                stt_[("UG", t)] = (bu, bg)

            def step_gelu(t):
                S.tag = "gelu(%d)" % t
                p = t % 2
                bu, bg = stt_[("UG", t)]
                tmpu, R_tmpu, tmpg, R_tmpg, gg, R_gg = tmpu_[p], R_tmpu_[p], tmpg_[p], R_tmpg_[p], gg_[p], R_gg_[p]
                act(tmpu, banks[bu], AF.Square, [Rb[bu]], [R_tmpu], scale=sq_scale)
                act(tmpg, banks[bg], AF.Square, [Rb[bg]], [R_tmpg], scale=sq_scale)
                stt(tmpu, tmpu, GK, banks[bu], ALU.add, ALU.mult, [R_tmpu, Rb[bu]], [R_tmpu])
                stt(tmpg, tmpg, GK, banks[bg], ALU.add, ALU.mult, [R_tmpg, Rb[bg]], [R_tmpg])
                act(tmpu, tmpu, AF.Exp, [R_tmpu], [R_tmpu], scale=-2.0)
                act(tmpg, tmpg, AF.Exp, [R_tmpg], [R_tmpg], scale=-2.0)
                act(tmpu, tmpu, AF.Ln, [R_tmpu], [R_tmpu], bias=1.0)
                act(tmpg, tmpg, AF.Ln, [R_tmpg], [R_tmpg], bias=1.0)
                act(tmpu, tmpu, AF.Exp, [R_tmpu], [R_tmpu], scale=-1.0)
                act(tmpg, tmpg, AF.Exp, [R_tmpg], [R_tmpg], scale=-1.0)
                tt(gu[p], tmpu, banks[bu], ALU.mult, [R_tmpu, Rb[bu]], [R_gu[p]])
                touch(bu)
                reserved.discard(bu)
                tt(gg, tmpg, banks[bg], ALU.mult, [R_tmpg, Rb[bg]], [R_gg])
                touch(bg)
                reserved.discard(bg)
                tt(tmpg, gg, gg, ALU.mult, [R_gg], [R_tmpg])
                S.add("dve", lambda e: e.tensor_reduce(out=ssh, in_=tmpg.rearrange("p (h d) -> p h d", h=8),
                                                       axis=AX.X, op=ALU.add), reads=[R_tmpg], writes=[R_ssh])
                tsc(ssh, ssh, 1.0 / 64, EPS, ALU.mult, ALU.add, [R_ssh], [R_ssh])
                tt(r8, ssh, mhalf_sb[:, 0:8], ALU.pow, [R_ssh, R_mhalf], [R_r8], eng="pool")

            def step_ghn(t):
                S.tag = "ghn(%d)" % t
                p = t % 2
                gg, R_gg = gg_[p], R_gg_[p]
                tt(gg.rearrange("p (h d) -> p h d", h=8), gg.rearrange("p (h d) -> p h d", h=8),
                   r8.unsqueeze(2).to_broadcast([128, 8, 64]), ALU.mult, [R_gg, R_r8], [R_gg])
                tt(ghn[p], gg, vgbc, ALU.mult, [R_gg, R_vg], [R_ghn[p]])

            def step_mix(i):
                S.tag = "mix(%d)" % i
                p = i % 2
                bm = nb()
                touch(bm)
                reserved.add(bm)
                for h in range(8):
                    c, ph = h // 2, h % 2
                    mm(banks[bm][ph * 64:(ph + 1) * 64, c * 128:(c + 1) * 128], ghn[p][:, h * 64:(h + 1) * 64],
                       wst[:, h, :], True, True, [R_ghn[p], R_wst], [Rb[bm]])
                stt_[("M", i)] = bm

            def step_gmlp_elt(i):
                S.tag = "gmlp_elt(%d)" % i
                p = i % 2
                bm = stt_[("M", i)]
                tt(mb, banks[bm], bh, ALU.add, [Rb[bm], R_bh], [R_mb])
                touch(bm)
                reserved.discard(bm)
                tt(cT[p].rearrange("p a b -> p (a b)"), mb, gu[p], ALU.mult, [R_mb, R_gu[p]], [R_cT[p]])

            def kts_of(i):
                return [kt for kt in (i - 1, i, i + 1) if 0 <= kt < T]

            def step_S(i):
                S.tag = "S(%d)" % i
                p = i % 2
                for j in range(2):
                    for s, kt in enumerate(kts_of(i)):
                        rel = kt - i
                        idx = (rel + 1) * 2 + j
                        bs = nb()
                        touch(bs)
                        mm(banks[bs], ident, Bhi[:, idx, :], True, False, [R_ident, R_E], [Rb[bs]])
                        for g in range(4):
                            c = 2 * j + g // 2
                            ph = g % 2
                            mm(banks[bs][:, g * 128:(g + 1) * 128], kT[:, j, kt * 128:(kt + 1) * 128],
                               (qTe if ph == 0 else qTo)[p][:, c, :], False, g == 3, [R_kT[kt], R_qT[p]], [Rb[bs]])
                        act(PT[j][:, s, :], banks[bs], AF.Exp, [Rb[bs]], [R_PT[j][s]], scale=0.125)
                        touch(bs)

            def step_PV(i):
                S.tag = "PV(%d)" % i
                p = i % 2
                kts = kts_of(i)
                bn = nb()
                touch(bn)
                bd = nb()
                touch(bd)
                for j in range(2):
                    for s, kt in enumerate(kts):
                        mm(banks[bn][j * 64:(j + 1) * 64, :], vp[:, kt, j * 64:(j + 1) * 64], PT[j][:, s, :],
                           s == 0, s == len(kts) - 1, [R_vp[kt], R_PT[j][s]], [Rb[bn]])
                    for s, kt in enumerate(kts):
                        mm(banks[bd][j * 64:(j + 1) * 64, :], vrep_sb[:, vbase + kt, :], PT[j][:, s, :],
                           s == 0, s == len(kts) - 1, [R_vrep, R_PT[j][s]], [Rb[bd]])
                act(numsb, banks[bn], AF.Copy, [Rb[bn]], [R_numsb])
                touch(bn)
                tt(rden, banks[bd], sexp2.rearrange("p a b -> p (a b)"), ALU.add, [Rb[bd], R_sexp2], [R_rden])
                touch(bd)
                act(rden, rden, AF.Ln, [R_rden], [R_rden])
                act(rden, rden, AF.Exp, [R_rden], [R_rden], scale=-1.0)

            def step_norm_finish(i):
                S.tag = "norm_finish(%d)" % i
                p = i % 2
                for j in range(2):
                    numv = numsb[j * 64:(j + 1) * 64, :].rearrange("p (c two q) -> p c two q", c=2, two=2)
                    rdv = rden[j * 64:(j + 1) * 64, :].rearrange("p (c two q) -> p c two q", c=2, two=2)
                    for ph in range(2):
                        tt(aTt[p][ph * 64:(ph + 1) * 64, 2 * j:2 * j + 2, :], numv[:, :, ph, :], rdv[:, :, ph, :],
                           ALU.mult, [R_numsb, R_rden], [R_aTt[p]])

            def step_Wout(i):
                S.tag = "Wout(%d)" % i
                p = i % 2
                for half in range(2):
                    bo = nb()
                    touch(bo)
                    for c in range(4):
                        mm(banks[bo], aTt[p][:, c, :], wout[:, c, half * 512:(half + 1) * 512], c == 0, False,
                           [R_aTt[p], R_wout], [Rb[bo]])
                    for c in range(4):
                        mm(banks[bo], cT[p][:, c, :], wout[:, 4 + c, half * 512:(half + 1) * 512], False, c == 3,
                           [R_cT[p], R_wout], [Rb[bo]])
                    xv = x_sb[:, i, half * 512:(half + 1) * 512]
                    tt(xv, xv, banks[bo], ALU.add, [R_x[i], Rb[bo]], [R_x[i]])
                    touch(bo)
                stats_tile(i, junk, R_junk)

            K = list(kv_tiles)
            outs = set(out_tiles)
            step_h(K[0])
            step_T(K[0])
            step_hTevac(K[0])
            if len(K) > 1:
                step_h(K[1])
            step_KV(K[0])
            for n in range(len(K) + 2):
                t1 = K[n] if n < len(K) else None
                t1n = K[n + 1] if n + 1 < len(K) else None
                tb = K[n - 1] if (1 <= n <= len(K) and K[n - 1] in outs) else None
                tw = K[n - 2] if (2 <= n <= len(K) + 1 and K[n - 2] in outs) else None
                order = cfg.get("mixorder", DEFAULT_MIXORDER)
                for tok in order:
                    if tok == "S" and tb is not None:
                        step_S(tb)
                    elif tok == "QUG" and t1 is not None and t1 in outs:
                        step_QUG(t1)
                    elif tok == "T" and t1n is not None:
                        step_T(t1n)
                    elif tok == "hTevac" and t1n is not None:
                        step_hTevac(t1n)
                    elif tok == "h" and t1n is not None and n + 2 < len(K):
                        step_h(K[n + 2])
                    elif tok == "KV" and t1n is not None:
                        step_KV(t1n)
                    elif tok == "normf" and tw is not None:
                        step_norm_finish(tw)
                    elif tok == "Wout" and tw is not None:
                        step_Wout(tw)
                    elif tok == "mix" and tb is not None:
                        step_mix(tb)
                    elif tok == "gelu" and t1 is not None and t1 in outs:
                        step_gelu(t1)
                    elif tok == "gmlp" and tb is not None:
                        step_gmlp_elt(tb)
                    elif tok == "PV" and tb is not None:
                        step_PV(tb)
                    elif tok == "ghn" and t1 is not None and t1 in outs:
                        step_ghn(t1)
            if cfg.get("listsched", True):
                S.schedule_window(win_start)

        def final_phase(own_tiles, y_d):
            S.barrier()
            A = Alloc(PH_BASE)
            gfin = A.get([128, D], F32)
            R_gfin = Res("gfin")
            junk = A.get([128, D], BF16)
            R_junk = Res("junk")
            yb = [A.get([128, D], F32) for _ in range(4)]
            R_yb = [Res("yb%d" % i) for i in range(4)]
            dma("sp", gfin, gfin_d.partition_broadcast(128), [], [R_gfin], st_c)
            norm_stats(own_tiles, junk, R_junk, compute=False)
            outs = []
            for n, t in enumerate(own_tiles):
                r = n % 4
                stt(yb[r], x_sb[:, t, :], rstd_sb[:, t:t + 1], gfin, ALU.mult, ALU.mult,
                    [R_x[t], R_rstd, R_gfin], [R_yb[r]])
                outs.append(dma("sp", y_d[n * 128:(n + 1) * 128, :], yb[r], [R_yb[r]], [], st_y[r]))
            return outs

        st_y = [S.stream("y%d" % i) for i in range(4)]

        def segment(x_d, T, vbase, plan, own_tiles, y_d):
            for gi, t0 in enumerate(range(0, T, 4)):
                dma("sp", x_sb[:, t0:t0 + 4, :],
                    x_d[t0 * 128:(t0 + 4) * 128, :].rearrange("(t p) d -> p t d", p=128),
                    [], [R_x[t] for t in range(t0, t0 + 4)], wstream("x%d" % gi))
            sg_name = 's' if T == TS else 'p'
            if PH is not None:
                for l in range(2):
                    f1, kv, ot, f2 = plan[l]
                    if (sg_name, l, 'f1') in PH:
                        ffn_phase(l, 0, f1, True, hook=(gen_bias_pieces if not st_bias["done"] else None))
                        st_bias["done"] = True
                    if (sg_name, l, 'mix') in PH:
                        if not st_bias["done"]:
                            gen_bias()
                            st_bias["done"] = True
                        S.barrier()
                        A0 = Alloc(PH_BASE)
                        jk = A0.get([128, D], BF16)
                        norm_stats(kv, jk, Res("jk"), compute=True)
                        mix_phase(l, T, vbase, kv, ot)
                    if (sg_name, l, 'f2') in PH:
                        ffn_phase(l, 1, f2, True)
                S.barrier()
                A0 = Alloc(PH_BASE)
                jk = A0.get([128, D], BF16)
                norm_stats(own_tiles, jk, Res("jk"), compute=True)
                return final_phase(own_tiles, y_d)
            R67 = Res("b67")
            for t in range(T):
                S.add("dve", (lambda t: lambda e: e.memset(ss_sb[:, t:t + 1], 0.0))(t), writes=[R_sst[t]], cost=0.1)
                act(b67, x_sb[:, t, :], AF.Square, [R_x[t], Rb[6], Rb[7]], [Rb[6], Rb[7], R_sst[t]],
                    accum_out=ss_sb[:, t:t + 1])
            for l in range(2):
                f1, kv, ot, f2 = plan[l]
                ffn_phase(l, 0, f1, False, hook=(gen_bias_pieces if not st_bias["done"] else None),
                          nobarrier=(l == 0 and not st_bias["done"]))
                st_bias["done"] = True
                mix_phase(l, T, vbase, kv, ot)
                ffn_phase(l, 1, f2, False)
            return final_phase(own_tiles, y_d)

        allS = list(range(TS))
        plan_s = [(allS, allS, allS, allS), (allS, allS, allS, allS)]
        plan_p = [(list(range(0, 20)), list(range(0, 20)), list(range(1, 19)), list(range(1, 19))),
                  (list(range(1, 19)), list(range(1, 19)), list(range(2, 18)), list(range(2, 18)))]
        st_bias = {"done": False}
        o1 = segment(xs_d, TS, 0, plan_s, allS, ys_d) if 's' in SEGS else []
        o2 = segment(xp_d, TP, TS, plan_p, list(range(2, 18)), yp_d) if 'p' in SEGS else []
        S.add("sp", None, extra_deps=o1[-4:] + o2[-4:])

        S.lower(lambda name: es.enter_context(nc.semaphore(name)))
        with nc.Block() as block:
            @block.tensor
            def _(e):
                S.emit("pe", e)

            @block.scalar
            def _(e):
                S.emit("act", e)

            @block.vector
            def _(e):
                S.emit("dve", e)

            @block.gpsimd
            def _(e):
                S.emit("pool", e)

            @block.sync
            def _(e):
                S.emit("sp", e)
    return nc, S


def _prep_inputs(x_prompt, x_sample, norm_ffn1, w1_gate, w1_up, w1_down, norm_mix, w_in, sink,
                 gmlp_v_gain, w_spatial, b_spatial, w_out, norm_ffn2, w2_gate, w2_up, w2_down, norm_final):
    f = lambda a: np.ascontiguousarray(np.asarray(a, dtype=np.float32))
    w_in = f(w_in)
    k0 = w_in[:, :, 512:576]
    k1 = w_in[:, :, 576:640]
    w_in_ext = np.concatenate([w_in[:, :, 0:512], k0, k0, k1, k1, w_in[:, :, 640:768],
                               w_in[:, :, 768:1280], w_in[:, :, 1280:1792]], axis=2)
    gl = lambda g: f(np.asarray(g, np.float32).reshape(2, 8, 128).transpose(0, 2, 1))
    w_sT = f(np.asarray(w_spatial, np.float32).transpose(0, 3, 1, 2))
    bs = np.asarray(b_spatial, np.float32).reshape(2, 4, 2, 128).transpose(0, 2, 1, 3)
    b_bc = f(np.repeat(bs, 64, axis=1).reshape(2, 128, 512))
    shared = {
        "w1_gate": f(w1_gate), "w1_up": f(w1_up), "w1_down": f(w1_down),
        "w2_gate": f(w2_gate), "w2_up": f(w2_up), "w2_down": f(w2_down),
        "g_ffn1": gl(norm_ffn1), "g_ffn2": gl(norm_ffn2), "g_mix": gl(norm_mix),
        "w_in_ext": f(w_in_ext), "w_out": f(w_out), "w_sT": w_sT, "b_bc": b_bc,
        "vgain": f(gmlp_v_gain), "sink": f(sink), "g_fin": f(np.asarray(norm_final, np.float32).reshape(1, D)),
    }
    xp = np.asarray(x_prompt, np.float32)
    xs = np.asarray(x_sample, np.float32)
    in_maps = []
    for c in range(8):
        b, qd = c // 4, c % 4
        lo = qd * 2048 - 256
        hi = (qd + 1) * 2048 + 256
        xpc = np.zeros((TP * 128, D), np.float32)
        vmask = np.zeros((TP * 128,), np.float32)
        a, e = max(lo, 0), min(hi, 8192)
        xpc[a - lo:e - lo] = xp[b, a:e]
        vmask[a - lo:e - lo] = 1.0
        valid = np.concatenate([np.ones((128, TS), np.float32), vmask.reshape(TP, 128).T], axis=1)
        m = dict(shared)
        m["xs"] = f(xs[c])
        m["xp"] = xpc
        m["valid"] = f(valid)
        in_maps.append(m)
    return in_maps


_CACHE = {}


def kernel(**inputs):
    in_maps = _prep_inputs(**inputs)
    if "nc" not in _CACHE:
        _CACHE["nc"] = build_program()[0]
    nc = _CACHE["nc"]
    res = run_bass_kernel_spmd(nc, in_maps, core_ids=list(range(8)))
    y_prompt = np.zeros((2, 8192, D), np.float32)
    y_sample = np.zeros((8, 2048, D), np.float32)
    for c in range(8):
        r = res.results[c]
        b, qd = c // 4, c % 4
        y_prompt[b, qd * 2048:(qd + 1) * 2048] = r["yp"]
        y_sample[c] = r["ys"]
    return (y_prompt, y_sample)
```
